# Optimizing a Trainium2 kernel written in Bass

```python
import math
import jax
import jax.numpy as jnp
from jax import lax
import numpy as np


D_MODEL = 1024
BATCH = 2
SEQ = 8192
DEPTH = 2

GRID_W = 64
CTX_LEN = 256
EPS = 1e-6
NEG_INF = -1e30
N_MOD = 6
D_FF = 4 * D_MODEL
ROPE_BASE = 10000.0

HY_WIDTH = D_MODEL // 2
HY_SHORT = 3
HY_EMB = 33
HY_BANDS = (HY_EMB - 1) // 2
HY_HIDDEN = 64
HY_TARGET = 1e-2
HY_FAST_PCT = 0.3
HY_SLOW_PCT = 1.5
HY_MIN_DECAY = -math.log(HY_TARGET) / HY_SLOW_PCT
HY_MAX_DECAY = -math.log(HY_TARGET) / HY_FAST_PCT

ATT_HEADS = 8
ATT_KV_HEADS = 2
ATT_GROUP = ATT_HEADS // ATT_KV_HEADS
HEAD_DIM = 64
ATT_WIDTH = ATT_HEADS * HEAD_DIM
KV_WIDTH = ATT_KV_HEADS * HEAD_DIM
WINDOW = 128
BLOCK = 128

EV_IN = 3 * HY_WIDTH + ATT_WIDTH + 2 * KV_WIDTH
EV_OUT = HY_WIDTH + ATT_WIDTH

RET_HEADS = 4
RET_DK = D_MODEL // RET_HEADS
RET_DV = 2 * RET_DK
RET_CHUNK = 128
RET_QK = RET_HEADS * RET_DK
RET_V = RET_HEADS * RET_DV
OD_IN = 2 * RET_QK + 3 * RET_V
OD_OUT = RET_V

N_EVEN = (DEPTH + 1) // 2
N_ODD = DEPTH // 2

kernel_name = 'hybrid_hyena_swa_retention_dit'


def rms_norm(x, g):
    xf = x.astype(jnp.float32)
    y = xf * lax.rsqrt(jnp.mean(xf * xf, axis=-1, keepdims=True) + EPS)
    return (y * g.astype(jnp.float32)).astype(x.dtype)


def modulate(h, shift, scale):
    return h * (1 + scale) + shift


def sq_relu_mlp(h, w1, w2):
    return jnp.square(jax.nn.relu(h @ w1)) @ w2


def rotate_half(x, cos, sin):
    x1, x2 = jnp.split(x, 2, axis=-1)
    return jnp.concatenate([x1 * cos - x2 * sin, x2 * cos + x1 * sin], axis=-1)


def rope_2d(x, pos_row, pos_col):
    half, quarter = HEAD_DIM // 2, HEAD_DIM // 4
    inv = ROPE_BASE ** (-jnp.arange(quarter, dtype=jnp.float32) / quarter)
    parts = []
    for xa, pos in ((x[..., :half], pos_row), (x[..., half:], pos_col)):
        ang = pos[:, None] * inv[None, :]
        parts.append(rotate_half(xa, jnp.cos(ang)[:, None, :].astype(x.dtype), jnp.sin(ang)[:, None, :].astype(x.dtype)))
    return jnp.concatenate(parts, axis=-1)


def rope_1d(x, pos):
    inv = ROPE_BASE ** (-jnp.linspace(0.0, 1.0, RET_DK // 2, dtype=jnp.float32))
    ang = pos[:, None] * inv[None, :]
    return rotate_half(x, jnp.cos(ang), jnp.sin(ang))


def short_conv(u, w, b):
    L = u.shape[1]
    up = jnp.pad(u, ((0, 0), (1, 1), (0, 0)))
    return up[:, :L] * w[0] + up[:, 1:L + 1] * w[1] + up[:, 2:] * w[2] + b


def hyena_filter_freq(L, w1, b1, w2, b2, w3, freq, decay):
    f32 = jnp.float32
    t = (jnp.arange(L, dtype=f32) / L)[:, None]
    bands = jnp.linspace(1e-4, HY_BANDS - 1, HY_BANDS, dtype=f32)
    ang = 2.0 * math.pi * t * bands[None, :]
    z = jnp.concatenate([t, jnp.cos(ang), -jnp.sin(ang)], axis=-1)
    fr = freq.astype(f32)
    hid = jnp.sin(fr * (z @ w1.astype(f32) + b1.astype(f32)))
    hid = jnp.sin(fr * (hid @ w2.astype(f32) + b2.astype(f32)))
    h = (hid @ w3.astype(f32)).reshape(L, 2, HY_WIDTH)
    h = h * jnp.exp(-t[:, :, None] * jnp.abs(decay.astype(f32))[None])
    h_pos, h_neg = h[:, 0], h[1:, 1]
    norm = jnp.sqrt(jnp.sum(h_pos * h_pos, axis=0) + jnp.sum(h_neg * h_neg, axis=0))
    h_circ = jnp.concatenate([h_pos, jnp.zeros((1, HY_WIDTH), f32), h_neg[::-1]], axis=0) / norm
    return jnp.fft.rfft(h_circ, axis=0)


def hyena(u, conv_w, conv_b, bias, filt):
    L = u.shape[1]
    x0, x1, v = jnp.split(short_conv(u, conv_w, conv_b), 3, axis=-1)
    z = v * x1
    hf = hyena_filter_freq(L, *filt)
    zf = jnp.fft.rfft(z.astype(jnp.float32), n=2 * L, axis=1)
    y = jnp.fft.irfft(zf * hf[None], n=2 * L, axis=1)[:, :L].astype(z.dtype)
    return x0 * (y + z * bias)


def softmax_with_sink(s, sink):
    col = jnp.broadcast_to(sink, s.shape[:-1] + (1,))
    return jax.nn.softmax(jnp.concatenate([s, col], axis=-1), axis=-1)[..., :-1]


def latent_window_attention(q, k, v, k_c, v_c, sink):
    B, S = q.shape[:2]
    nb = S // BLOCK
    n_win = 3 * BLOCK
    scale = HEAD_DIM ** -0.5
    qb = q.reshape(B, nb, BLOCK, ATT_KV_HEADS, ATT_GROUP, HEAD_DIM)

    def band(t):
        tp = jnp.pad(t, ((0, 0), (BLOCK, BLOCK), (0, 0), (0, 0))).reshape(B, nb + 2, BLOCK, ATT_KV_HEADS, HEAD_DIM)
        return jnp.concatenate([tp[:, :-2], tp[:, 1:-1], tp[:, 2:]], axis=2)

    kw, vw = band(k), band(v)
    q_off = jnp.arange(BLOCK)[:, None]
    k_off = jnp.arange(n_win)[None, :] - BLOCK
    key_pos = jnp.arange(nb)[:, None, None] * BLOCK + k_off[None]
    valid = (jnp.abs(q_off - k_off) <= WINDOW)[None] & (key_pos >= 0) & (key_pos < S)
    s_loc = jnp.einsum('bnqkgd,bnskd->bkgnqs', qb, kw).astype(jnp.float32) * scale
    s_loc = jnp.where(valid, s_loc, NEG_INF)
    s_ctx = jnp.einsum('bnqkgd,bckd->bkgnqc', qb, k_c).astype(jnp.float32) * scale
    sink_b = sink.astype(jnp.float32).reshape(1, ATT_KV_HEADS, ATT_GROUP, 1, 1, 1)
    p = softmax_with_sink(jnp.concatenate([s_loc, s_ctx], axis=-1), sink_b).astype(v.dtype)
    o = (jnp.einsum('bkgnqs,bnskd->bnqkgd', p[..., :n_win], vw)
         + jnp.einsum('bkgnqc,bckd->bnqkgd', p[..., n_win:], v_c))
    return o.reshape(B, S, ATT_WIDTH)


def context_attention(q_c, k_c, v_c, sink):
    B, Lc = q_c.shape[:2]
    qg = q_c.reshape(B, Lc, ATT_KV_HEADS, ATT_GROUP, HEAD_DIM)
    s = jnp.einsum('bqkgd,bskd->bkgqs', qg, k_c).astype(jnp.float32) * HEAD_DIM ** -0.5
    p = softmax_with_sink(s, sink.astype(jnp.float32).reshape(1, ATT_KV_HEADS, ATT_GROUP, 1, 1)).astype(v_c.dtype)
    return jnp.einsum('bkgqs,bskd->bqkgd', p, v_c).reshape(B, Lc, ATT_WIDTH)


def even_mixer(h_c, h_l, w_in, w_out, conv_w, conv_b, hy_w1, hy_b1, hy_w2, hy_b2, hy_w3, hy_freq,
               hy_decay, hy_bias, sink, pos_row, pos_col, need_ctx_out):
    filt = (hy_w1, hy_b1, hy_w2, hy_b2, hy_w3, hy_freq, hy_decay)
    i_q = 3 * HY_WIDTH
    i_k = i_q + ATT_WIDTH
    i_v = i_k + KV_WIDTH

    def heads(p):
        B, L, _ = p.shape
        return (p[..., i_q:i_k].reshape(B, L, ATT_HEADS, HEAD_DIM),
                p[..., i_k:i_v].reshape(B, L, ATT_KV_HEADS, HEAD_DIM),
                p[..., i_v:].reshape(B, L, ATT_KV_HEADS, HEAD_DIM))

    p_l = h_l @ w_in
    p_c = h_c @ w_in
    q_l, k_l, v_l = heads(p_l)
    q_c, k_c, v_c = heads(p_c)
    q_l = rope_2d(q_l, pos_row, pos_col)
    k_l = rope_2d(k_l, pos_row, pos_col)
    y_l = jnp.concatenate([hyena(p_l[..., :i_q], conv_w, conv_b, hy_bias, filt),
                           latent_window_attention(q_l, k_l, v_l, k_c, v_c, sink)], axis=-1) @ w_out
    y_c = None
    if need_ctx_out:
        y_c = jnp.concatenate([hyena(p_c[..., :i_q], conv_w, conv_b, hy_bias, filt),
                               context_attention(q_c, k_c, v_c, sink)], axis=-1) @ w_out
    return y_c, y_l


def chunk_retention(q, k, v, state, log_gamma):
    B, H, L, _ = q.shape
    n = L // RET_CHUNK
    idx = jnp.arange(RET_CHUNK, dtype=jnp.float32)
    diff = idx[:, None] - idx[None, :]
    dmask = jnp.exp(jnp.where(diff >= 0, diff[None] * log_gamma[:, None, None], -jnp.inf))
    xi = jnp.exp((idx + 1)[None, :] * log_gamma[:, None])[..., None]
    zeta = jnp.exp((RET_CHUNK - 1 - idx)[None, :] * log_gamma[:, None])[..., None]
    g_chunk = jnp.exp(RET_CHUNK * log_gamma)[:, None, None]

    def to_chunks(t):
        return t.reshape(B, H, n, RET_CHUNK, t.shape[-1]).transpose(2, 0, 1, 3, 4)

    def step(R, inp):
        qn, kn, vn = inp
        inner = jnp.einsum('bhid,bhjd->bhij', qn, kn) * dmask
        o = jnp.einsum('bhij,bhje->bhie', inner, vn) + jnp.einsum('bhid,bhde->bhie', qn * xi, R)
        R = g_chunk * R + jnp.einsum('bhjd,bhje->bhde', kn * zeta, vn)
        return R, o

    state, o = lax.scan(step, state, (to_chunks(q), to_chunks(k), to_chunks(v)))
    return o.transpose(1, 2, 0, 3, 4).reshape(B, H, L, RET_DV), state


def retention_direction(q_c, k_c, v_c, q_l, k_l, v_l, log_gamma):
    B, H, Lc = q_c.shape[:3]
    Ll = q_l.shape[2]
    pos_c = jnp.arange(Lc, dtype=jnp.float32)
    pos_l = Lc + jnp.arange(Ll, dtype=jnp.float32)
    state0 = jnp.zeros((B, H, RET_DK, RET_DV), jnp.float32)
    o_c, state_c = chunk_retention(rope_1d(q_c, pos_c), rope_1d(k_c, pos_c), v_c, state0, log_gamma)
    o_l, _ = chunk_retention(rope_1d(q_l, pos_l), rope_1d(k_l, pos_l), v_l, state_c, log_gamma)
    return o_c, o_l


def head_rms(o):
    o = o * lax.rsqrt(jnp.mean(o * o, axis=-1, keepdims=True) + EPS)
    B, H, L, dv = o.shape
    return o.transpose(0, 2, 1, 3).reshape(B, L, H * dv)


def retention_mixer(h_c, h_l, w_in, w_out, log_rate, need_ctx_out):
    def split(p):
        B, L, _ = p.shape
        f32 = jnp.float32
        q = p[..., :RET_QK].reshape(B, L, RET_HEADS, RET_DK).transpose(0, 2, 1, 3).astype(f32)
        k = p[..., RET_QK:2 * RET_QK].reshape(B, L, RET_HEADS, RET_DK).transpose(0, 2, 1, 3).astype(f32) * RET_DK ** -0.5
        v = p[..., 2 * RET_QK:2 * RET_QK + RET_V].reshape(B, L, RET_HEADS, RET_DV).transpose(0, 2, 1, 3).astype(f32)
        g_f = p[..., 2 * RET_QK + RET_V:2 * RET_QK + 2 * RET_V]
        g_b = p[..., 2 * RET_QK + 2 * RET_V:]
        return q, k, v, g_f, g_b

    q_l, k_l, v_l, gf_l, gb_l = split(h_l @ w_in)
    q_c, k_c, v_c, gf_c, gb_c = split(h_c @ w_in)
    log_gamma = -jnp.exp(log_rate.astype(jnp.float32))

    def flip(t):
        return t[:, :, ::-1]

    of_c, of_l = retention_direction(q_c, k_c, v_c, q_l, k_l, v_l, log_gamma[0])
    ob_c, ob_l = retention_direction(flip(q_c), flip(k_c), flip(v_c), flip(q_l), flip(k_l), flip(v_l), log_gamma[1])

    def merge(o_f, o_b, g_f, g_b):
        y = jax.nn.silu(g_f) * head_rms(o_f).astype(g_f.dtype) + jax.nn.silu(g_b) * head_rms(o_b).astype(g_b.dtype)
        return y @ w_out

    y_l = merge(of_l, flip(ob_l), gf_l, gb_l)
    y_c = None
    if need_ctx_out:
        y_c = merge(of_c, flip(ob_c), gf_c, gb_c)
    return y_c, y_l


def setup_inputs(seed: int = 0) -> dict:
    key = jax.random.key(seed)
    ks = jax.random.split(key, 32)
    D = D_MODEL

    def nrm(k, shape, scale):
        return jax.random.normal(k, shape, jnp.float32) * scale

    hy_base = jnp.linspace(HY_MIN_DECAY, HY_MAX_DECAY, HY_WIDTH, dtype=jnp.float32)
    ret_base = -(5.0 + jnp.arange(RET_HEADS, dtype=jnp.float32)) * math.log(2.0)
    return {
        'x': nrm(ks[0], (BATCH, SEQ, D), 1.0),
        'c': nrm(ks[1], (BATCH, D), 1.0),
        'ctx': nrm(ks[2], (BATCH, CTX_LEN, D), 1.0),
        'c_ctx': nrm(ks[3], (D,), 1.0),
        'ada_w': nrm(ks[4], (DEPTH, D, N_MOD * D), 0.5 * D ** -0.5),
        'ada_b': nrm(ks[5], (DEPTH, N_MOD * D), 0.01),
        'norm_mix_g': 1.0 + nrm(ks[6], (DEPTH, D), 0.02),
        'norm_mlp_g': 1.0 + nrm(ks[7], (DEPTH, D), 0.02),
        'mlp_w1': nrm(ks[8], (DEPTH, D, D_FF), D ** -0.5),
        'mlp_w2': nrm(ks[9], (DEPTH, D_FF, D), D_FF ** -0.5),
        'ev_w_in': nrm(ks[10], (N_EVEN, D, EV_IN), D ** -0.5),
        'ev_w_out': nrm(ks[11], (N_EVEN, EV_OUT, D), EV_OUT ** -0.5),
        'hy_conv_w': nrm(ks[12], (N_EVEN, HY_SHORT, 3 * HY_WIDTH), HY_SHORT ** -0.5),
        'hy_conv_b': nrm(ks[13], (N_EVEN, 3 * HY_WIDTH), 0.02),
        'hy_w1': nrm(ks[14], (N_EVEN, HY_EMB, HY_HIDDEN), HY_EMB ** -0.5),
        'hy_b1': nrm(ks[15], (N_EVEN, HY_HIDDEN), 0.1),
        'hy_w2': nrm(ks[16], (N_EVEN, HY_HIDDEN, HY_HIDDEN), HY_HIDDEN ** -0.5),
        'hy_b2': nrm(ks[17], (N_EVEN, HY_HIDDEN), 0.1),
        'hy_w3': nrm(ks[18], (N_EVEN, HY_HIDDEN, 2 * HY_WIDTH), HY_HIDDEN ** -0.5),
        'hy_freq': 1.0 + nrm(ks[19], (N_EVEN, HY_HIDDEN), 0.1),
        'hy_decay': hy_base * (1.0 + nrm(ks[20], (N_EVEN, 2, HY_WIDTH), 0.05)),
        'hy_bias': nrm(ks[21], (N_EVEN, HY_WIDTH), 1.0),
        'attn_sink': nrm(ks[22], (N_EVEN, ATT_HEADS), 0.5),
        'od_w_in': nrm(ks[23], (N_ODD, D, OD_IN), D ** -0.5),
        'od_w_out': nrm(ks[24], (N_ODD, OD_OUT, D), OD_OUT ** -0.5),
        'ret_log_rate': ret_base + nrm(ks[25], (N_ODD, 2, RET_HEADS), 0.05),
        'final_g': 1.0 + nrm(ks[26], (D,), 0.02),
    }


def reference(x, c, ctx, c_ctx, ada_w, ada_b, norm_mix_g, norm_mlp_g, mlp_w1, mlp_w2,
              ev_w_in, ev_w_out, hy_conv_w, hy_conv_b, hy_w1, hy_b1, hy_w2, hy_b2, hy_w3, hy_freq,
              hy_decay, hy_bias, attn_sink, od_w_in, od_w_out, ret_log_rate, final_g):
    n_tok = x.shape[1]
    rows = n_tok // GRID_W
    grid_r, grid_c = jnp.meshgrid(jnp.arange(rows, dtype=jnp.float32),
                                  jnp.arange(GRID_W, dtype=jnp.float32), indexing='ij')
    pos_row = grid_r.reshape(-1)
    pos_col = grid_c.reshape(-1)
    c_act = jax.nn.silu(c)
    cc_act = jax.nn.silu(c_ctx)

    for i in range(DEPTH):
        need_ctx_out = i < DEPTH - 1
        mod_l = (c_act @ ada_w[i] + ada_b[i])[:, None, :]
        mod_c = cc_act @ ada_w[i] + ada_b[i]
        sh_a, sc_a, g_a, sh_m, sc_m, g_m = jnp.split(mod_l, N_MOD, axis=-1)
        csh_a, csc_a, cg_a, csh_m, csc_m, cg_m = jnp.split(mod_c, N_MOD, axis=-1)
        h_l = modulate(rms_norm(x, norm_mix_g[i]), sh_a, sc_a)
        h_c = modulate(rms_norm(ctx, norm_mix_g[i]), csh_a, csc_a)
        j = i // 2
        if i % 2 == 0:
            y_c, y_l = even_mixer(h_c, h_l, ev_w_in[j], ev_w_out[j], hy_conv_w[j], hy_conv_b[j],
                                  hy_w1[j], hy_b1[j], hy_w2[j], hy_b2[j], hy_w3[j], hy_freq[j],
                                  hy_decay[j], hy_bias[j], attn_sink[j], pos_row, pos_col, need_ctx_out)
        else:
            y_c, y_l = retention_mixer(h_c, h_l, od_w_in[j], od_w_out[j], ret_log_rate[j], need_ctx_out)
        x = x + g_a * y_l
        x = x + g_m * sq_relu_mlp(modulate(rms_norm(x, norm_mlp_g[i]), sh_m, sc_m), mlp_w1[i], mlp_w2[i])
        if need_ctx_out:
            ctx = ctx + cg_a * y_c
            ctx = ctx + cg_m * sq_relu_mlp(modulate(rms_norm(ctx, norm_mlp_g[i]), csh_m, csc_m), mlp_w1[i], mlp_w2[i])

    return rms_norm(x, final_g)
```

```python
import numpy as np
import ml_dtypes
from contextlib import ExitStack
import concourse.bass as bass
import concourse.mybir as mybir
from concourse.bass_utils import run_bass_kernel_spmd

F32 = mybir.dt.float32
BF16 = mybir.dt.bfloat16
AF = mybir.ActivationFunctionType
ALU = mybir.AluOpType
AX = mybir.AxisListType
NPBF = ml_dtypes.bfloat16

ENGS = ("sync", "scalar", "vector", "gpsimd", "tensor")
NCORES = 8
D = 1024
S = 8192
LC = 256
EPS = 1e-6
TL = 2048
TC = 64
TT = TL + TC


class Buf:
    __slots__ = ("name", "lw", "rd")

    def __init__(self, name=""):
        self.name = name
        self.lw = None
        self.rd = {}


class Prog:
    NDMA = 24

    def __init__(self, nc, strict_same=True):
        self.nc = nc
        self.stack = ExitStack()
        self.ops = {e: [] for e in ENGS}
        self.sems = {}
        self.cnt = {}
        for e in ("scalar", "vector", "gpsimd", "tensor"):
            self.sems[e] = self.stack.enter_context(nc.semaphore("s_" + e))
            self.cnt[e] = 0
        for i in range(self.NDMA):
            k = "d%d" % i
            self.sems[k] = self.stack.enter_context(nc.semaphore("s_" + k))
            self.cnt[k] = 0
        self.rr = 0
        self.seen = {e: {} for e in ENGS}
        self.strict_same = strict_same
        self.ntile = 0
        self.ARENA = 48000
        self.arena = self.stack.enter_context(nc.sbuf_tensor("arena", [128, self.ARENA], F32))
        self.psum = self.stack.enter_context(nc.psum_tensor("psum_all", [128, 4096], F32))
        self.off = 0
        self.poff = 0
        self.epoch = 0

    def sb(self, shape, dtype=F32, name=None):
        n = int(np.prod(shape[1:]))
        esz = 2 if dtype == BF16 else 4
        words = (n * esz + 3) // 4
        words = (words + 7) // 8 * 8
        assert self.off + words <= self.ARENA, "SBUF arena overflow %d + %d" % (self.off, words)
        ap = self.arena[0:shape[0], self.off:self.off + words]
        self.off += words
        if dtype != F32:
            ap = ap.bitcast(dtype)
        ap = ap[:, 0:n]
        if len(shape) > 2:
            names = " ".join("d%d" % i for i in range(len(shape) - 1))
            ap = ap.rearrange("p (%s) -> p %s" % (names, names), **{"d%d" % i: shape[i + 1] for i in range(len(shape) - 1)})
        return ap

    def ps(self, shape, dtype=F32, name=None):
        n = int(np.prod(shape[1:]))
        assert dtype == F32
        if n > 512:
            assert n % 512 == 0
        if (self.poff % 512) + n > 512 and self.poff % 512:
            self.poff = (self.poff // 512 + 1) * 512
        assert self.poff + n <= 8 * 512, "PSUM overflow"
        ap = self.psum[0:shape[0], self.poff:self.poff + n]
        self.poff += n
        if len(shape) > 2:
            names = " ".join("d%d" % i for i in range(len(shape) - 1))
            ap = ap.rearrange("p (%s) -> p %s" % (names, names), **{"d%d" % i: shape[i + 1] for i in range(len(shape) - 1)})
        return ap

    def _waits(self, eng, r, w, extra=()):
        need = {}

        def add(tok):
            if tok is None:
                return
            k, v = tok
            if need.get(k, 0) < v:
                need[k] = v
        for b in r:
            add(b.lw)
        for b in w:
            add(b.lw)
            for k, v in b.rd.items():
                add((k, v))
        for tok in extra:
            add(tok)
        out = []
        seen = self.seen[eng]
        for k, v in need.items():
            if k == eng and (eng == "tensor" or not self.strict_same):
                continue
            if seen.get(k, 0) >= v:
                continue
            seen[k] = v
            out.append((self.sems[k], v))
        return out

    def _commit(self, tok, r, w):
        k, v = tok
        for b in w:
            b.lw = tok
            b.rd = {}
        for b in r:
            if b.rd.get(k, 0) < v:
                b.rd[k] = v

    def op(self, eng, fn, r=(), w=()):
        waits = self._waits(eng, r, w)
        self.cnt[eng] += 1
        tok = (eng, self.cnt[eng])
        sem = self.sems[eng]

        def emit(e, waits=waits, fn=fn, sem=sem):
            for s, v in waits:
                e.wait_ge(s, v)
            fn(e).then_inc(sem, 1)
        self.ops[eng].append(emit)
        self._commit(tok, r, w)
        return tok

    def dma(self, q, out, in_, r=(), w=(), **kw):
        k = "d%d" % self.rr
        self.rr = (self.rr + 1) % self.NDMA
        prev = (k, self.cnt[k]) if self.cnt[k] else None
        waits = self._waits(q, r, w, extra=(prev,) if prev else ())
        self.cnt[k] += 16
        tok = (k, self.cnt[k])
        sem = self.sems[k]

        def emit(e, waits=waits, sem=sem, out=out, in_=in_, kw=kw):
            for s, v in waits:
                e.wait_ge(s, v)
            e.dma_start(out=out, in_=in_, **kw).then_inc(sem, 16)
        self.ops[q].append(emit)
        self._commit(tok, r, w)
        return tok

    def wait_all(self, eng, toks):
        waits = self._waits(eng, (), (), extra=toks)

        def emit(e, waits=waits):
            for s, v in waits:
                e.wait_ge(s, v)
        self.ops[eng].append(emit)

    def barrier(self):
        toks = self.all_tokens()
        for e in ENGS:
            self.wait_all(e, toks)
        self.off = 0
        self.poff = 0
        self.epoch += 1

    def all_tokens(self):
        return [(k, v) for k, v in self.cnt.items() if v]

    def flush(self):
        self.barrier()

    def finish(self, out_toks):
        self.wait_all("sync", out_toks)
        nc = self.nc
        ops = self.ops
        with nc.Block() as block:
            @block.sync
            def _(e):
                for f in ops["sync"]:
                    f(e)

            @block.scalar
            def _(e):
                for f in ops["scalar"]:
                    f(e)

            @block.vector
            def _(e):
                for f in ops["vector"]:
                    f(e)

            @block.gpsimd
            def _(e):
                for f in ops["gpsimd"]:
                    f(e)

            @block.tensor
            def _(e):
                for f in ops["tensor"]:
                    f(e)
        self.stack.close()


class Consts:
    def __init__(self, P):
        self.identf = P.sb([128, 128], F32)
        self.identb = P.sb([128, 128], BF16)
        self.onesb = P.sb([128, 128], BF16)
        self.b = Buf("consts")
        P.op("gpsimd", lambda e: e.memset(self.identf[:], 1.0), w=[self.b])
        P.op("gpsimd", lambda e: e.affine_select(out=self.identf[:], in_=self.identf[:], pattern=[[-1, 128]],
                                                 compare_op=ALU.is_equal, fill=0.0, base=0, channel_multiplier=1),
             r=[self.b], w=[self.b])
        P.op("gpsimd", lambda e: e.tensor_copy(out=self.identb[:], in_=self.identf[:]), r=[self.b], w=[self.b])
        P.op("gpsimd", lambda e: e.memset(self.onesb[:], 1.0), w=[self.b])


class NormCtx:
    def __init__(self, P, T, nbuf=2):
        self.T = T
        self.sq = [P.sb([128, 8, T], BF16) for _ in range(nbuf)]
        self.bsq = [Buf() for _ in range(nbuf)]
        self.pss = [P.ps([128, T], F32) for _ in range(nbuf)]
        self.bps = [Buf() for _ in range(nbuf)]
        self.rstd = [P.sb([128, T], F32) for _ in range(nbuf)]
        self.brs = [Buf() for _ in range(nbuf)]
        self.tmp = [P.sb([128, T], F32) for _ in range(2)]
        self.btmp = [Buf() for _ in range(2)]
        self.i = 0
        self.k = 0


def norm_mod(P, C, N, xT, bx, T, A, Bv, bAB, outT, bout):
    i = N.i
    N.i = (N.i + 1) % len(N.sq)
    sq, bsq, pss, bps, rstd, brs = N.sq[i], N.bsq[i], N.pss[i], N.bps[i], N.rstd[i], N.brs[i]
    for j in range(8):
        P.op("scalar", lambda e, j=j: e.activation(out=sq[:, j, :T], in_=xT(j), func=AF.Square), r=[bx], w=[bsq])
    for j in range(8):
        P.op("tensor", lambda e, j=j: e.matmul(pss[:, :T], lhsT=C.onesb[:], rhs=sq[:, j, :T], start=(j == 0), stop=(j == 7)),
             r=[bsq, C.b], w=[bps])
    P.op("scalar", lambda e: e.activation(out=rstd[:, :T], in_=pss[:, :T], func=AF.Sqrt, bias=N.epsb[:, 0:1], scale=1.0 / D),
         r=[bps, N.beps], w=[brs])
    P.op("vector", lambda e: e.reciprocal(out=rstd[:, :T], in_=rstd[:, :T]), r=[brs], w=[brs])
    for j in range(8):
        k = N.k
        N.k = (N.k + 1) % 2
        tmp, btmp = N.tmp[k], N.btmp[k]
        if Bv is None:
            P.op("vector", lambda e, j=j: e.scalar_tensor_tensor(out=outT(j), in0=xT(j), scalar=A(j), in1=rstd[:, :T],
                                                                 op0=ALU.mult, op1=ALU.mult),
                 r=[bx, brs, bAB], w=[bout])
        else:
            P.op("vector", lambda e, j=j, tmp=tmp: e.scalar_tensor_tensor(out=tmp[:, :T], in0=xT(j), scalar=A(j), in1=rstd[:, :T],
                                                                          op0=ALU.mult, op1=ALU.mult),
                 r=[bx, brs, bAB], w=[btmp])
            P.op("scalar", lambda e, j=j, tmp=tmp: e.activation(out=outT(j), in_=tmp[:, :T], func=AF.Identity, bias=Bv(j), scale=1.0),
                 r=[btmp, bAB], w=[bout])


def make_eps(P, N):
    N.epsb = P.sb([128, 1], F32)
    N.beps = Buf()
    P.op("gpsimd", lambda e: e.memset(N.epsb[:], EPS), w=[N.beps])


def load_cast_weight(P, w_dram, KC, M, wb, bwb, stage, bstage, col0=0, eng_cycle=("vector", "gpsimd"), cw=1024, st=[0]):
    for kc in range(KC):
        for c0 in range(0, M, cw):
            cc = min(cw, M - c0)
            s = st[0] % len(stage)
            st[0] += 1
            P.dma("sync", stage[s][:, :cc], w_dram[kc * 128:(kc + 1) * 128, col0 + c0:col0 + c0 + cc], w=[bstage[s]])
            eng = eng_cycle[st[0] % len(eng_cycle)]
            if eng == "scalar":
                P.op("scalar", lambda e, s=s, kc=kc, c0=c0, cc=cc: e.copy(out=wb[:, kc, c0:c0 + cc], in_=stage[s][:, :cc]),
                     r=[bstage[s]], w=[bwb])
            else:
                P.op(eng, lambda e, s=s, kc=kc, c0=c0, cc=cc: e.tensor_copy(out=wb[:, kc, c0:c0 + cc], in_=stage[s][:, :cc]),
                     r=[bstage[s]], w=[bwb])


def build_L1():
    nc = bass.Bass("TRN2", target_bir_lowering=False)
    x = nc.dram_tensor("x", [TT, D], F32, kind="ExternalInput").ap()
    cT = nc.dram_tensor("cT", [128, 8, 2], F32, kind="ExternalInput").ap()
    ada_w = nc.dram_tensor("ada_w", [2, D, 6 * D], F32, kind="ExternalInput").ap()
    ada_bT = nc.dram_tensor("ada_bT", [128, 2, 48], F32, kind="ExternalInput").ap()
    gT = nc.dram_tensor("gT", [128, 5, 8], F32, kind="ExternalInput").ap()
    xT_o = nc.dram_tensor("xT_o", [128, 8, TT], F32, kind="ExternalOutput").ap()
    modT_o = nc.dram_tensor("modT_o", [128, 2, 48, 2], F32, kind="ExternalOutput").ap()
    hT_o = nc.dram_tensor("hT_o", [128, 8, TT], BF16, kind="ExternalOutput").ap()
    P = Prog(nc)
    C = Consts(P)
    N = NormCtx(P, 512)
    make_eps(P, N)
    outs = []
    cs = P.sb([128, 8, 2]); bcs = Buf()
    ca = P.sb([128, 8, 2]); bca = Buf()
    abT = P.sb([128, 2, 48]); bab = Buf()
    gs = P.sb([128, 5, 8]); bgs = Buf()
    modT = P.sb([128, 2, 48, 2]); bmod = Buf()
    P.dma("sync", cs[:], cT, w=[bcs])
    P.dma("sync", abT[:], ada_bT, w=[bab])
    P.dma("sync", gs[:], gT, w=[bgs])
    P.op("scalar", lambda e: e.activation(out=ca[:], in_=cs[:], func=AF.Silu), r=[bcs], w=[bca])
    stg = [P.sb([128, 8, 512]) for _ in range(2)]
    bstg = [Buf() for _ in range(2)]
    pm = [P.ps([128, 4, 2]) for _ in range(2)]
    bpm = [Buf() for _ in range(2)]
    it = 0
    for l in range(2):
        for g in range(12):
            s = it % 2
            it += 1
            P.dma("sync" if it % 2 else "gpsimd", stg[s][:], ada_w[l, :, g * 512:(g + 1) * 512].rearrange("(kc p) n -> p kc n", p=128), w=[bstg[s]])
            for jj in range(4):
                for kc in range(8):
                    P.op("tensor", lambda e, s=s, jj=jj, kc=kc: e.matmul(pm[s][:, jj, :], lhsT=stg[s][:, kc, jj * 128:(jj + 1) * 128],
                                                                        rhs=ca[:, kc, :], start=(kc == 0), stop=(kc == 7)),
                         r=[bstg[s], bca], w=[bpm[s]])
            for r_ in range(2):
                P.op("vector", lambda e, s=s, l=l, g=g, r_=r_: e.tensor_tensor(out=modT[:, l, g * 4:(g + 1) * 4, r_], in0=pm[s][:, :, r_],
                                                                              in1=abT[:, l, g * 4:(g + 1) * 4], op=ALU.add),
                     r=[bpm[s], bab], w=[bmod])
    outs.append(P.dma("sync", modT_o, modT[:], r=[bmod]))
    A = P.sb([128, 8, 2]); bA = Buf()
    for r_ in range(2):
        P.op("vector", lambda e, r_=r_: e.scalar_tensor_tensor(out=A[:, :, r_], in0=modT[:, 0, 8:16, r_], scalar=1.0, in1=gs[:, 0, :],
                                                               op0=ALU.add, op1=ALU.mult), r=[bmod, bgs], w=[bA])
    xT = P.sb([128, 8, TT]); bxT = [Buf() for _ in range(5)]
    hT = P.sb([128, 8, TT], BF16); bhT = [Buf() for _ in range(5)]
    xin = [P.sb([128, D]) for _ in range(3)]
    bxin = [Buf() for _ in range(3)]
    ptr = [P.ps([128, 8, 128]) for _ in range(2)]
    bptr = [Buf() for _ in range(2)]
    ntile = (TT + 127) // 128
    for t in range(ntile):
        rows = min(128, TT - t * 128)
        s = t % 3
        P.dma("sync" if t % 2 else "gpsimd", xin[s][:rows, :], x[t * 128:t * 128 + rows, :], w=[bxin[s]])
        pp = t % 2
        for j in range(8):
            P.op("tensor", lambda e, s=s, pp=pp, j=j, rows=rows: e.transpose(out=ptr[pp][:, j, :rows], in_=xin[s][:rows, j * 128:(j + 1) * 128],
                                                                          identity=C.identf[:rows, :rows]),
                 r=[bxin[s], C.b], w=[bptr[pp]])
        big = t // 4
        eng = "vector" if t % 2 else "scalar"
        if eng == "vector":
            P.op("vector", lambda e, pp=pp, t=t, rows=rows: e.tensor_copy(out=xT[:, :, t * 128:t * 128 + rows], in_=ptr[pp][:, :, :rows]),
                 r=[bptr[pp]], w=[bxT[big]])
        else:
            P.op("scalar", lambda e, pp=pp, t=t, rows=rows: e.copy(out=xT[:, :, t * 128:t * 128 + rows], in_=ptr[pp][:, :, :rows]),
                 r=[bptr[pp]], w=[bxT[big]])
    for big in range(5):
        t0 = big * 512
        T = min(512, TT - t0)
        r_ = 0 if big < 4 else 1
        outs.append(P.dma("gpsimd", xT_o[:, :, t0:t0 + T], xT[:, :, t0:t0 + T], r=[bxT[big]]))
        norm_mod(P, C, N, lambda j, t0=t0, T=T: xT[:, j, t0:t0 + T], bxT[big], T,
                 lambda j, r_=r_: A[:, j, r_:r_ + 1], lambda j, r_=r_: modT[:, 0, j, r_:r_ + 1], bA,
                 lambda j, t0=t0, T=T: hT[:, j, t0:t0 + T], bhT[big])
        outs.append(P.dma("sync", hT_o[:, :, t0:t0 + T], hT[:, :, t0:t0 + T], r=[bhT[big]]))
    P.finish(outs)
    return nc


def core_bq(i):
    return i // 4, i % 4


def fm(vec, kc):
    return np.ascontiguousarray(np.asarray(vec).reshape(kc, 128).T)


def prep_L1(inp):
    x, ctx, c, c_ctx = inp["x"], inp["ctx"], inp["c"], inp["c_ctx"]
    ada_bT = np.ascontiguousarray(inp["ada_b"].reshape(2, 48, 128).transpose(2, 0, 1))
    gs = np.stack([inp["norm_mix_g"][0], inp["norm_mlp_g"][0], inp["norm_mix_g"][1], inp["norm_mlp_g"][1], inp["final_g"]], 0)
    gT = np.ascontiguousarray(gs.reshape(5, 8, 128).transpose(2, 0, 1))
    maps = []
    for i in range(NCORES):
        b, q = core_bq(i)
        xc = np.concatenate([x[b, q * TL:(q + 1) * TL], ctx[b, q * TC:(q + 1) * TC]], 0)
        cT = np.stack([fm(c[b], 8), fm(c_ctx, 8)], -1)
        maps.append({"x": np.ascontiguousarray(xc), "cT": np.ascontiguousarray(cT), "ada_w": inp["ada_w"], "ada_bT": ada_bT, "gT": gT})
    return maps


def phase_outproj(P, nc, l, KC, TTx, has_ctx, xT_in, mixT, w_out, modT_d, xmid):
    C = Consts(P)
    modT = P.sb([128, 2, 48, 2]); bmod = Buf()
    P.dma("sync", modT[:], modT_d, w=[bmod])
    wb = P.sb([128, KC, D], BF16); bwb = Buf()
    stage = [P.sb([128, 1024]) for _ in range(2)]
    bstage = [Buf() for _ in range(2)]
    load_cast_weight(P, w_out, KC, D, wb, bwb, stage, bstage)
    NT = 512
    xs = [P.sb([128, 8, NT]) for _ in range(2)]; bxs = [Buf() for _ in range(2)]
    ms = [P.sb([128, KC, NT], BF16) for _ in range(2)]; bms = [Buf() for _ in range(2)]
    py = [P.ps([128, NT]) for _ in range(2)]; bpy = [Buf() for _ in range(2)]
    tiles = [(t0, min(NT, TL - t0), 0) for t0 in range(0, TL, NT)]
    if has_ctx:
        tiles.append((TL, TC, 1))
    toks = []
    k = 0
    for it, (t0, T, r_) in enumerate(tiles):
        s = it % 2
        P.dma("sync", xs[s][:, :, :T], xT_in[:, :, t0:t0 + T], w=[bxs[s]])
        P.dma("gpsimd", ms[s][:, :, :T], mixT[:, :, t0:t0 + T], w=[bms[s]])
        for j in range(8):
            pp = k % 2
            k += 1
            for kc in range(KC):
                P.op("tensor", lambda e, pp=pp, kc=kc, j=j, s=s, T=T: e.matmul(py[pp][:, :T], lhsT=wb[:, kc, j * 128:(j + 1) * 128], rhs=ms[s][:, kc, :T],
                                                                             start=(kc == 0), stop=(kc == KC - 1)),
                     r=[bwb, bms[s]], w=[bpy[pp]])
            P.op("vector", lambda e, pp=pp, j=j, s=s, T=T, r_=r_: e.scalar_tensor_tensor(out=xs[s][:, j, :T], in0=py[pp][:, :T], scalar=modT[:, l, 16 + j, r_:r_ + 1],
                                                                                        in1=xs[s][:, j, :T], op0=ALU.mult, op1=ALU.add),
                 r=[bpy[pp], bmod, bxs[s]], w=[bxs[s]])
        toks.append(P.dma("sync", xmid[:, :, t0:t0 + T], xs[s][:, :, :T], r=[bxs[s]]))
    return toks


def phase_mlp(P, nc, l, TTx, has_ctx, final, xmid, w1, w2, modT_d, gT_d, xT_o, hT_o, out_o):
    C = Consts(P)
    NT = 256
    N = NormCtx(P, NT, nbuf=1)
    make_eps(P, N)
    modT = P.sb([128, 2, 48, 2]); bmod = Buf()
    gs = P.sb([128, 5, 8]); bgs = Buf()
    P.dma("sync", modT[:], modT_d, w=[bmod])
    P.dma("sync", gs[:], gT_d, w=[bgs])
    Am = P.sb([128, 8, 2]); bAm = Buf()
    An = P.sb([128, 8, 2]); bAn = Buf()
    for r_ in range(2):
        P.op("vector", lambda e, r_=r_: e.scalar_tensor_tensor(out=Am[:, :, r_], in0=modT[:, l, 32:40, r_], scalar=1.0, in1=gs[:, 2 * l + 1, :],
                                                               op0=ALU.add, op1=ALU.mult), r=[bmod, bgs], w=[bAm])
        if not final:
            P.op("vector", lambda e, r_=r_: e.scalar_tensor_tensor(out=An[:, :, r_], in0=modT[:, l + 1, 8:16, r_], scalar=1.0, in1=gs[:, 2 * (l + 1), :],
                                                                   op0=ALU.add, op1=ALU.mult), r=[bmod, bgs], w=[bAn])
    w1b = P.sb([128, 8, 4 * D], BF16); bw1 = Buf()
    w2b = P.sb([128, 32, D], BF16); bw2 = Buf()
    stage = [P.sb([128, 1024]) for _ in range(2)]
    bstage = [Buf() for _ in range(2)]
    load_cast_weight(P, w1, 8, 4 * D, w1b, bw1, stage, bstage)
    load_cast_weight(P, w2, 32, D, w2b, bw2, stage, bstage)
    xs0 = P.sb([128, 8, NT]); bxs0 = Buf()
    xs = [xs0, xs0]; bxs = [bxs0, bxs0]
    h2 = P.sb([128, 8, NT], BF16); bh2 = Buf()
    u = P.sb([128, 32, NT], BF16); bu = Buf()
    rl = [P.sb([128, NT]) for _ in range(2)]; brl = [Buf() for _ in range(2)]
    if final:
        zb = P.sb([128, 1])
        P.op("gpsimd", lambda e: e.memset(zb, 0.0), w=[bgs])
        yT = P.sb([128, 8, NT]); byT = Buf()
    else:
        h1 = P.sb([128, 8, NT], BF16); bh1 = Buf()
    pu = [P.ps([128, NT]) for _ in range(2)]; bpu = [Buf() for _ in range(2)]
    pd = [P.ps([128, NT]) for _ in range(2)]; bpd = [Buf() for _ in range(2)]
    tiles = [(t0, min(NT, TL - t0), 0) for t0 in range(0, TL, NT)]
    if has_ctx:
        tiles.append((TL, TC, 1))
    toks = []
    ku = 0
    kd = 0
    ko = 0
    for it, (t0, T, r_) in enumerate(tiles):
        s = it % 2
        P.dma("sync", xs[s][:, :, :T], xmid[:, :, t0:t0 + T], w=[bxs[s]])
        norm_mod(P, C, N, lambda j, s=s, T=T: xs[s][:, j, :T], bxs[s], T,
                 lambda j, r_=r_: Am[:, j, r_:r_ + 1], lambda j, r_=r_: modT[:, l, 24 + j, r_:r_ + 1], bAm,
                 lambda j, T=T: h2[:, j, :T], bh2)
        for c in range(32):
            pp = ku % 2
            ku += 1
            for kc in range(8):
                P.op("tensor", lambda e, pp=pp, kc=kc, c=c, T=T: e.matmul(pu[pp][:, :T], lhsT=w1b[:, kc, c * 128:(c + 1) * 128], rhs=h2[:, kc, :T],
                                                                        start=(kc == 0), stop=(kc == 7)),
                     r=[bw1, bh2], w=[bpu[pp]])
            P.op("scalar", lambda e, pp=pp, T=T: e.activation(out=rl[pp][:, :T], in_=pu[pp][:, :T], func=AF.Relu), r=[bpu[pp]], w=[brl[pp]])
            eng = "vector" if c % 2 else "gpsimd"
            P.op(eng, lambda e, pp=pp, c=c, T=T: e.tensor_tensor(out=u[:, c, :T], in0=rl[pp][:, :T], in1=rl[pp][:, :T], op=ALU.mult),
                 r=[brl[pp]], w=[bu])
        for j in range(8):
            pp = kd % 2
            kd += 1
            for c in range(32):
                P.op("tensor", lambda e, pp=pp, c=c, j=j, T=T: e.matmul(pd[pp][:, :T], lhsT=w2b[:, c, j * 128:(j + 1) * 128], rhs=u[:, c, :T],
                                                                       start=(c == 0), stop=(c == 31)),
                     r=[bw2, bu], w=[bpd[pp]])
            P.op("vector", lambda e, pp=pp, j=j, s=s, T=T, r_=r_: e.scalar_tensor_tensor(out=xs[s][:, j, :T], in0=pd[pp][:, :T], scalar=modT[:, l, 40 + j, r_:r_ + 1],
                                                                                        in1=xs[s][:, j, :T], op0=ALU.mult, op1=ALU.add),
                 r=[bpd[pp], bmod, bxs[s]], w=[bxs[s]])
        if not final:
            toks.append(P.dma("gpsimd", xT_o[:, :, t0:t0 + T], xs[s][:, :, :T], r=[bxs[s]]))
            norm_mod(P, C, N, lambda j, s=s, T=T: xs[s][:, j, :T], bxs[s], T,
                     lambda j, r_=r_: An[:, j, r_:r_ + 1], lambda j, r_=r_: modT[:, l + 1, j, r_:r_ + 1], bAn,
                     lambda j, T=T: h1[:, j, :T], bh1)
            toks.append(P.dma("gpsimd", hT_o[:, :, t0:t0 + T], h1[:, :, :T], r=[bh1]))
        else:
            norm_mod(P, C, N, lambda j, s=s, T=T: xs[s][:, j, :T], bxs[s], T,
                     lambda j: gs[:, 4, j:j + 1], lambda j: zb[:, 0:1], bgs,
                     lambda j, T=T: yT[:, j, :T], byT)
            toks.append(P.dma("gpsimd", out_o[:, :, t0:t0 + T], yT[:, :, :T], r=[byT]))
    return toks


def build_L35(l, KC, has_ctx, final):
    nc = bass.Bass("TRN2", target_bir_lowering=False)
    TTx = TT if has_ctx else TL
    xT_in = nc.dram_tensor("xT", [128, 8, TTx], F32, kind="ExternalInput").ap()
    mixT = nc.dram_tensor("mixT", [128, KC, TTx], BF16, kind="ExternalInput").ap()
    w_out = nc.dram_tensor("w_out", [KC * 128, D], F32, kind="ExternalInput").ap()
    w1 = nc.dram_tensor("w1", [D, 4 * D], F32, kind="ExternalInput").ap()
    w2 = nc.dram_tensor("w2", [4 * D, D], F32, kind="ExternalInput").ap()
    modT_d = nc.dram_tensor("modT", [128, 2, 48, 2], F32, kind="ExternalInput").ap()
    gT_d = nc.dram_tensor("gT", [128, 5, 8], F32, kind="ExternalInput").ap()
    xmid = nc.dram_tensor("xmid", [128, 8, TTx], F32).ap()
    xT_o = hT_o = out_o = None
    if final:
        out_o = nc.dram_tensor("y_out", [128, 8, TL], F32, kind="ExternalOutput").ap()
    else:
        xT_o = nc.dram_tensor("xT_o", [128, 8, TTx], F32, kind="ExternalOutput").ap()
        hT_o = nc.dram_tensor("hT_o", [128, 8, TTx], BF16, kind="ExternalOutput").ap()
    P = Prog(nc)
    phase_outproj(P, nc, l, KC, TTx, has_ctx, xT_in, mixT, w_out, modT_d, xmid)
    P.flush()
    toks = phase_mlp(P, nc, l, TTx, has_ctx, final, xmid, w1, w2, modT_d, gT_d, xT_o, hT_o, out_o)
    P.finish(toks)
    return nc


NF = 2 * S
HB = 16
TWO_PI = 2.0 * np.pi


def hy_embed(L):
    t = (np.arange(L, dtype=np.float32) / np.float32(L))[:, None]
    bands = np.linspace(1e-4, HB - 1, HB, dtype=np.float32)
    ang = (np.float32(2.0 * np.pi) * t * bands[None, :]).astype(np.float32)
    z = np.concatenate([t, np.cos(ang), -np.sin(ang)], -1).astype(np.float32)
    return z, t[:, 0]


def l2_consts():
    c = {}
    zL, tL_ = hy_embed(S)
    n = np.arange(NF)
    idx = np.where(n < S, n, NF - n)
    idx[S] = 0
    c["zembT_big"] = np.ascontiguousarray(zL[idx].T)
    c["t_big"] = np.ascontiguousarray(np.broadcast_to(tL_[idx][None, :], (64, NF))).astype(np.float32)
    zc, tc = hy_embed(LC)
    idc = np.concatenate([np.arange(LC), np.arange(LC)])
    c["zembT_small"] = np.ascontiguousarray(zc[idc].T)
    c["t_small"] = np.ascontiguousarray(np.broadcast_to(tc[idc][None, :], (128, 2 * LC))).astype(np.float32)
    a = np.arange(128, dtype=np.float64)
    ang = 2.0 * np.pi * np.outer(a, a) / 128.0
    Cm, Sm = np.cos(ang), np.sin(ang)
    FaD = np.zeros((128, 256))
    FaD[:64, :128] = Cm[:64]; FaD[:64, 128:] = -Sm[:64]
    FaD[64:, :128] = Sm[:64]; FaD[64:, 128:] = Cm[:64]
    FaH = np.concatenate([Cm, -Sm], 1)
    ang2 = 2.0 * np.pi * np.outer(a, a) / NF
    Tr, Ti = np.cos(ang2), -np.sin(ang2)
    Gc1 = np.concatenate([Cm, Sm], 1)
    Gc2 = np.concatenate([-Sm, Cm], 1)
    Fd1 = np.concatenate([Cm[:, :64], Sm[:, :64]], 1)
    Fd2 = np.concatenate([-Sm[:, :64], Cm[:, :64]], 1)
    mats = np.stack([FaD[:, :128], FaD[:, 128:], FaH[:, :128], FaH[:, 128:], Cm, Sm, -Sm, Gc1[:, :128], Gc1[:, 128:],
                     Gc2[:, :128], Gc2[:, 128:], Fd1, Fd2], 1)
    c["dftm"] = np.ascontiguousarray(mats).astype(np.float32)
    c["twid"] = np.ascontiguousarray(np.stack([Tr, Ti], 1)).astype(np.float32)
    pos = np.arange(S)
    prow = (pos // 64).astype(np.float32)
    pcol = (pos % 64).astype(np.float32)
    inv = (10000.0 ** (-np.arange(16, dtype=np.float32) / 16)).astype(np.float32)
    cosT = np.zeros((64, S), np.float32)
    sinT = np.zeros((64, S), np.float32)
    for d in range(64):
        p_ = prow if d < 32 else pcol
        angd = (p_ * inv[d % 16]).astype(np.float32)
        cosT[d] = np.cos(angd)
        sg = -1.0 if (d % 32) < 16 else 1.0
        sinT[d] = sg * np.sin(angd)
    c["rope"] = np.ascontiguousarray(np.stack([cosT, sinT], 1))
    i_ = np.arange(128)[:, None]
    j_ = np.arange(384)[None, :]
    c["maskband"] = np.where((j_ - i_ >= 0) & (j_ - i_ <= 256), 0.0, -1e9).astype(np.float32)
    return c


def rope2d_partner():
    d = np.arange(64)
    return np.where((d % 32) < 16, d + 16, d - 16)


def prep_L2(inp, hTl, hTc):
    c = l2_consts()
    w_in = inp["ev_w_in"][0]
    part = rope2d_partner()
    maps = []
    for i in range(NCORES):
        kv = i // 4
        ch = np.arange(64) + 64 * i
        qc = 1536 + 64 * i + np.arange(64)
        kc_ = 2048 + 64 * kv + np.arange(64)
        vc = 2176 + 64 * kv + np.arange(64)
        cols = np.concatenate([ch, 512 + ch, 1024 + ch, qc, qc[part], kc_, kc_[part], vc])
        m = dict(c)
        m["w_all"] = np.ascontiguousarray(w_in[:, cols])
        cw = inp["hy_conv_w"][0]
        cb = inp["hy_conv_b"][0]
        cwc = np.stack([cw[:, s * 512 + ch].T for s in range(3)], 1)
        cbc = np.stack([cb[s * 512 + ch] for s in range(3)], 1)
        m["convw"] = np.ascontiguousarray(np.concatenate([cwc, cwc], 0)).astype(np.float32)
        m["convb"] = np.ascontiguousarray(np.concatenate([cbc, cbc], 0)).astype(np.float32)
        hb = inp["hy_bias"][0][ch][:, None]
        m["hbias"] = np.ascontiguousarray(np.concatenate([hb, hb], 0)).astype(np.float32)
        dec = inp["hy_decay"][0][:, ch].T
        m["decay"] = np.ascontiguousarray(np.concatenate([dec, dec], 0)).astype(np.float32)
        m["hw1"] = np.ascontiguousarray(inp["hy_w1"][0])
        m["hw2"] = np.ascontiguousarray(inp["hy_w2"][0])
        w3 = inp["hy_w3"][0]
        w3c = np.stack([w3[:, ch], w3[:, 512 + ch]], 1)
        m["hw3"] = np.ascontiguousarray(np.concatenate([w3c, w3c], 2))
        m["hvec"] = np.ascontiguousarray(np.stack([inp["hy_b1"][0], inp["hy_b2"][0], inp["hy_freq"][0]], 1))
        m["sink"] = np.full((128, 1), inp["attn_sink"][0][i], np.float32)
        m["hTl"] = hTl
        m["hTc"] = hTc
        maps.append(m)
    return maps


class FFTCtx:
    pass


def l2_fft_setup(P, dftm_d, twid_d, inverse=True):
    Fx = FFTCtx()
    Fx.bc = Buf()
    st = P.sb([128, 13, 128])
    P.dma("sync", st, dftm_d, w=[Fx.bc])
    Fx.dft = P.sb([128, 13, 128], BF16)
    P.op("vector", lambda e: e.tensor_copy(out=Fx.dft, in_=st), r=[Fx.bc], w=[Fx.bc])
    Fx.tw = P.sb([128, 2, 128])
    P.dma("sync", Fx.tw, twid_d, w=[Fx.bc])
    G = 4
    Fx.G = G
    Fx.pa = P.ps([128, G, 256]); Fx.bpa = Buf()
    Fx.pxr = P.ps([128, G, 128]); Fx.pxi = P.ps([128, G, 128]); Fx.bpx = Buf()
    if inverse:
        Fx.pb = P.ps([128, G, 256]); Fx.bpb = Buf()
        Fx.py = P.ps([128, G, 128]); Fx.bpy = Buf()
    Fx.t = [P.sb([128, G, 128]) for _ in range(4)]; Fx.bt = [Buf() for _ in range(4)]
    Fx.ar = P.sb([128, G, 128], BF16); Fx.ai = P.sb([128, G, 128], BF16); Fx.ba = Buf()
    Fx.yr = P.sb([128, G, 128], BF16); Fx.yi = P.sb([128, G, 128], BF16); Fx.by = Buf()
    Fx.br = P.sb([128, G, 128], BF16); Fx.bi = P.sb([128, G, 128], BF16); Fx.bb = Buf()
    return Fx


def cmul(P, Fx, xr, xi, bx, wr, wi, bw, outr, outi, bo, conj=False):
    t = Fx.t
    bt = Fx.bt
    P.op("vector", lambda e: e.tensor_tensor(out=t[0], in0=xr, in1=wr, op=ALU.mult), r=[bx, bw], w=[bt[0]])
    P.op("vector", lambda e: e.tensor_tensor(out=t[1], in0=xi, in1=wi, op=ALU.mult), r=[bx, bw], w=[bt[1]])
    P.op("vector", lambda e: e.tensor_tensor(out=t[2], in0=xr, in1=wi, op=ALU.mult), r=[bx, bw], w=[bt[2]])
    P.op("vector", lambda e: e.tensor_tensor(out=t[3], in0=xi, in1=wr, op=ALU.mult), r=[bx, bw], w=[bt[3]])
    if not conj:
        P.op("gpsimd", lambda e: e.tensor_tensor(out=outr, in0=t[0], in1=t[1], op=ALU.subtract), r=[bt[0], bt[1]], w=[bo])
        P.op("gpsimd", lambda e: e.tensor_tensor(out=outi, in0=t[2], in1=t[3], op=ALU.add), r=[bt[2], bt[3]], w=[bo])
    else:
        P.op("gpsimd", lambda e: e.tensor_tensor(out=outr, in0=t[0], in1=t[1], op=ALU.add), r=[bt[0], bt[1]], w=[bo])
        P.op("gpsimd", lambda e: e.tensor_tensor(out=outi, in0=t[3], in1=t[2], op=ALU.subtract), r=[bt[2], bt[3]], w=[bo])


def fft_fwd(P, Fx, Zg, bz, c0, rhsA):
    G = Fx.G
    d = Fx.dft
    for g in range(G):
        P.op("tensor", lambda e, g=g: e.matmul(Fx.pa[:, g, :], lhsT=Zg[:, c0 + g, :], rhs=rhsA, start=True, stop=True),
             r=[bz, Fx.bc], w=[Fx.bpa])
    twr = Fx.tw[:, 0:1, :].to_broadcast([128, G, 128])
    twi = Fx.tw[:, 1:2, :].to_broadcast([128, G, 128])
    cmul(P, Fx, Fx.pa[:, :, 0:128], Fx.pa[:, :, 128:256], Fx.bpa, twr, twi, Fx.bc, Fx.ar, Fx.ai, Fx.ba)
    ar = Fx.ar.rearrange("p g k -> p (g k)")
    ai = Fx.ai.rearrange("p g k -> p (g k)")
    pxr = Fx.pxr.rearrange("p g k -> p (g k)")
    pxi = Fx.pxi.rearrange("p g k -> p (g k)")
    P.op("tensor", lambda e: e.matmul(pxr, lhsT=d[:, 4, :], rhs=ar, start=True, stop=False), r=[Fx.ba, Fx.bc], w=[Fx.bpx])
    P.op("tensor", lambda e: e.matmul(pxr, lhsT=d[:, 5, :], rhs=ai, start=False, stop=True), r=[Fx.ba, Fx.bc], w=[Fx.bpx])
    P.op("tensor", lambda e: e.matmul(pxi, lhsT=d[:, 4, :], rhs=ai, start=True, stop=False), r=[Fx.ba, Fx.bc], w=[Fx.bpx])
    P.op("tensor", lambda e: e.matmul(pxi, lhsT=d[:, 6, :], rhs=ar, start=False, stop=True), r=[Fx.ba, Fx.bc], w=[Fx.bpx])


def hy_filter_mlp(P, zembT_d, t_d, npos, hw1, hw2, hw3, fr, c1, c2, negpi, nad, bvec, side_of_tile, M, hf, bhf):
    if getattr(P, "_fm", None) is None or P._fm[0] != id(P.arena) or P._fm[1] != P.epoch:
        zt = [P.sb([33, 512]) for _ in range(2)]; bzt = [Buf() for _ in range(2)]
        tt = [P.sb([128, 512]) for _ in range(2)]; btt = [Buf() for _ in range(2)]
        p1 = P.ps([64, 512]); bp1 = Buf()
        p3 = P.ps([128, 512]); bp3 = Buf()
        a1 = P.sb([64, 512]); ba1 = Buf()
        h1 = P.sb([64, 512]); bh1 = Buf()
        h2 = P.sb([64, 512]); bh2 = Buf()
        dec = P.sb([128, 512]); bdec = Buf()
        ki = P.sb([64, 512]).bitcast(mybir.dt.int32); bki = Buf()
        P._fm = (id(P.arena), P.epoch, (zt, bzt, tt, btt, p1, bp1, p3, bp3, a1, ba1, h1, bh1, h2, bh2, dec, bdec, ki, bki))
    zt, bzt, tt, btt, p1, bp1, p3, bp3, a1, ba1, h1, bh1, h2, bh2, dec, bdec, ki, bki = P._fm[2]
    for it in range(npos // 512):
        s = it % 2
        p0 = it * 512
        P.dma("sync", zt[s], zembT_d[:, p0:p0 + 512], w=[bzt[s]])
        P.dma("gpsimd", tt[s][:M, :], t_d[:, p0:p0 + 512], w=[btt[s]])
        P.op("tensor", lambda e, s=s: e.matmul(p1, lhsT=hw1, rhs=zt[s], start=True, stop=True), r=[bzt[s], bvec], w=[bp1])
        P.op("vector", lambda e: e.tensor_scalar(out=a1, in0=p1, scalar1=fr, scalar2=c1, op0=ALU.mult, op1=ALU.add), r=[bp1, bvec], w=[ba1])
        P.op("vector", lambda e: e.tensor_scalar(out=ki, in0=a1, scalar1=float(1.0 / TWO_PI), scalar2=None, op0=ALU.mult), r=[ba1], w=[bki])
        P.op("vector", lambda e: e.scalar_tensor_tensor(out=a1, in0=ki, scalar=float(-TWO_PI), in1=a1, op0=ALU.mult, op1=ALU.add), r=[ba1, bki], w=[ba1])
        P.op("vector", lambda e: e.tensor_scalar(out=a1, in0=a1, scalar1=-3.14159, scalar2=3.14159, op0=ALU.max, op1=ALU.min), r=[ba1], w=[ba1])
        P.op("scalar", lambda e: e.activation(out=h1, in_=a1, func=AF.Sin), r=[ba1, bvec], w=[bh1])
        P.op("tensor", lambda e: e.matmul(p1, lhsT=hw2, rhs=h1, start=True, stop=True), r=[bh1, bvec], w=[bp1])
        P.op("vector", lambda e: e.tensor_scalar(out=a1, in0=p1, scalar1=fr, scalar2=c2, op0=ALU.mult, op1=ALU.add), r=[bp1, bvec], w=[ba1])
        P.op("vector", lambda e: e.tensor_scalar(out=ki, in0=a1, scalar1=float(1.0 / TWO_PI), scalar2=None, op0=ALU.mult), r=[ba1], w=[bki])
        P.op("vector", lambda e: e.scalar_tensor_tensor(out=a1, in0=ki, scalar=float(-TWO_PI), in1=a1, op0=ALU.mult, op1=ALU.add), r=[ba1, bki], w=[ba1])
        P.op("vector", lambda e: e.tensor_scalar(out=a1, in0=a1, scalar1=-3.14159, scalar2=3.14159, op0=ALU.max, op1=ALU.min), r=[ba1], w=[ba1])
        P.op("scalar", lambda e: e.activation(out=h2, in_=a1, func=AF.Sin), r=[ba1, bvec], w=[bh2])
        for hh in range(2):
            sd = side_of_tile(it, hh)
            cs = slice(hh * 256, (hh + 1) * 256)
            P.op("tensor", lambda e, sd=sd, cs=cs: e.matmul(p3[:M, cs], lhsT=hw3[:, sd, :M], rhs=h2[:, cs], start=True, stop=True),
                 r=[bh2, bvec], w=[bp3])
            P.op("scalar", lambda e, sd=sd, cs=cs, s=s: e.activation(out=dec[:M, cs], in_=tt[s][:M, cs], func=AF.Exp, scale=nad[:M, sd:sd + 1]),
                 r=[btt[s], bvec], w=[bdec])
        P.op("vector", lambda e, p0=p0: e.tensor_tensor(out=hf[:M, p0:p0 + 512], in0=p3[:M, :], in1=dec[:M, :], op=ALU.mult),
             r=[bp3, bdec], w=[bhf])


def hy_filter_norm(P, hf, M, npos, bhf, zero_pos):
    ss = P.sb([128, 1]); bss = Buf()
    junk = P.sb([128, 2048]); bj = Buf()
    P.op("gpsimd", lambda e: e.memset(hf[:M, zero_pos:zero_pos + 1], 0.0), r=[], w=[bhf])
    nchunk = (npos + 2047) // 2048
    acc = P.sb([128, 8]); bacc = Buf()
    P.op("gpsimd", lambda e: e.memset(acc, 0.0), w=[bacc])
    for ci in range(nchunk):
        n0 = ci * 2048
        n1 = min(npos, n0 + 2048)
        P.op("scalar", lambda e, n0=n0, n1=n1, ci=ci: e.activation(out=junk[:M, :n1 - n0], in_=hf[:M, n0:n1], func=AF.Square,
                                                                   accum_out=acc[:M, ci:ci + 1]), r=[bhf, bacc], w=[bj, bacc])
    P.op("vector", lambda e: e.reduce_sum(out=ss[:M, :], in_=acc[:M, :], axis=AX.X), r=[bacc], w=[bss])
    P.op("scalar", lambda e: e.activation(out=ss[:M, :], in_=ss[:M, :], func=AF.Sqrt), r=[bss], w=[bss])
    P.op("vector", lambda e: e.reciprocal(out=ss[:M, :], in_=ss[:M, :]), r=[bss], w=[bss])
    for ci in range(nchunk):
        n0 = ci * 2048
        n1 = min(npos, n0 + 2048)
        eng = "vector" if ci % 2 else "gpsimd"
        P.op(eng, lambda e, n0=n0, n1=n1: e.tensor_scalar(out=hf[:M, n0:n1], in0=hf[:M, n0:n1], scalar1=ss[:M, 0:1], scalar2=None, op0=ALU.mult),
             r=[bhf, bss], w=[bhf])


def l2_filter_pass(P, nc, D_, hf_d, hs_d, hb_d, dbg=None):
    Fx = l2_fft_setup(P, D_["dftm"], D_["twid"], inverse=False)
    bvec = Buf()
    hw1 = P.sb([33, 64]); hw2 = P.sb([64, 64]); hw3 = P.sb([64, 2, 128]); hvec = P.sb([64, 3]); decay = P.sb([128, 2])
    for t_, d_ in ((hw1, "hw1"), (hw2, "hw2"), (hw3, "hw3"), (hvec, "hvec"), (decay, "decay")):
        P.dma("sync", t_, D_[d_], w=[bvec])
    cv = P.sb([128, 8])
    OFFS = 0.0
    P.op("vector", lambda e: e.tensor_scalar(out=cv[:64, 0:2], in0=hvec[:, 0:2], scalar1=hvec[:, 2:3], scalar2=OFFS, op0=ALU.mult, op1=ALU.add),
         r=[bvec], w=[bvec])
    P.op("gpsimd", lambda e: e.memset(cv[:, 2:3], float(-np.pi)), w=[bvec])
    P.op("scalar", lambda e: e.activation(out=cv[:, 3:5], in_=decay, func=AF.Abs), r=[bvec], w=[bvec])
    P.op("vector", lambda e: e.tensor_scalar(out=cv[:, 3:5], in0=cv[:, 3:5], scalar1=-1.0, scalar2=None, op0=ALU.mult), r=[bvec], w=[bvec])
    fr = hvec[:, 2:3]
    hs = P.sb([128, 512]); bhs = Buf()
    hy_filter_mlp(P, D_["zembT_small"], D_["t_small"], 512, hw1, hw2, hw3, fr, cv[:64, 0:1], cv[:64, 1:2], cv[:64, 2:3], cv[:, 3:5], bvec,
                  lambda it, hh: hh, 128, hs, bhs)
    hy_filter_norm(P, hs, 128, 512, bhs, 256)
    P.dma("sync", hs_d, hs, r=[bhs])
    hf = P.sb([64, NF]); bhf = Buf()
    hy_filter_mlp(P, D_["zembT_big"], D_["t_big"], NF, hw1, hw2, hw3, fr, cv[:64, 0:1], cv[:64, 1:2], cv[:64, 2:3], cv[:, 3:5], bvec,
                  lambda it, hh: 0 if it < 16 else 1, 64, hf, bhf)
    hy_filter_norm(P, hf, 64, NF, bhf, S)
    bhb = Buf()
    P.dma("sync", hb_d, hf, r=[bhf], w=[bhb])
    if dbg is not None:
        P.dma("sync", dbg["hfilt"], hf, r=[bhf])
    G = Fx.G
    zst = [P.sb([128, G, 128]) for _ in range(2)]; bzst = [Buf() for _ in range(2)]
    zg = [P.sb([128, G, 128], BF16) for _ in range(2)]; bzg = [Buf() for _ in range(2)]
    xo = [P.sb([128, G, 2, 128]) for _ in range(2)]; bxo = [Buf() for _ in range(2)]
    toks = []
    for sg in range(64 // G):
        s = sg % 2
        c0 = sg * G
        P.dma("sync", zst[s], hb_d[c0:c0 + G, :].rearrange("c (n1 n2) -> n1 c n2", n2=128), r=[bhb], w=[bzst[s]])
        P.op("scalar", lambda e, s=s: e.copy(out=zg[s], in_=zst[s]), r=[bzst[s]], w=[bzg[s]])
        fft_fwd(P, Fx, zg[s], bzg[s], 0, Fx.dft[:, 2:4, :].rearrange("p a k -> p (a k)"))
        P.op("scalar", lambda e, s=s: e.copy(out=xo[s][:, :, 0, :], in_=Fx.pxr), r=[Fx.bpx], w=[bxo[s]])
        P.op("scalar", lambda e, s=s: e.copy(out=xo[s][:, :, 1, :], in_=Fx.pxi), r=[Fx.bpx], w=[bxo[s]])
        toks.append(P.dma("gpsimd", hf_d[:, c0:c0 + G, :, :], xo[s], r=[bxo[s]]))
    return toks


def short_conv_inplace(P, u, bu, L, cw, cb, bcv, tmp, btmp, carry, bcar, CH=2048):
    for t0 in range(0, L, CH):
        t1 = min(L, t0 + CH)
        n = t1 - t0
        k = (t0 // CH) % 2
        if t1 < L:
            P.op("gpsimd", lambda e, k=k, t1=t1: e.tensor_copy(out=carry[:, 1 - k:2 - k], in_=u[:, t1 - 1:t1]), r=[bu], w=[bcar])
        P.op("scalar", lambda e, t0=t0, t1=t1, n=n: e.activation(out=tmp[:, :n], in_=u[:, t0:t1], func=AF.Identity, bias=cb, scale=cw[:, 1:2]),
             r=[bu, bcv], w=[btmp])
        n2 = n if t1 < L else n - 1
        P.op("vector", lambda e, t0=t0, n2=n2: e.scalar_tensor_tensor(out=tmp[:, :n2], in0=u[:, t0 + 1:t0 + 1 + n2], scalar=cw[:, 2:3], in1=tmp[:, :n2],
                                                                     op0=ALU.mult, op1=ALU.add), r=[bu, bcv, btmp], w=[btmp])
        P.op("vector", lambda e, t0=t0, n=n: e.scalar_tensor_tensor(out=tmp[:, 1:n], in0=u[:, t0:t0 + n - 1], scalar=cw[:, 0:1], in1=tmp[:, 1:n],
                                                                   op0=ALU.mult, op1=ALU.add), r=[bu, bcv, btmp], w=[btmp])
        if t0 > 0:
            P.op("vector", lambda e, k=k: e.scalar_tensor_tensor(out=tmp[:, 0:1], in0=carry[:, k:k + 1], scalar=cw[:, 0:1], in1=tmp[:, 0:1],
                                                                 op0=ALU.mult, op1=ALU.add), r=[bcar, bcv, btmp], w=[btmp])
        P.op("gpsimd", lambda e, t0=t0, t1=t1, n=n: e.tensor_copy(out=u[:, t0:t1], in_=tmp[:, :n]), r=[btmp, bcar], w=[bu])


def l2_hyena_pass(P, nc, D_, hf_d, hs_d, zs_d, ys_d, mix_hy, mix_hyc, dbg=None):
    C = Consts(P)
    Fx = l2_fft_setup(P, D_["dftm"], D_["twid"], inverse=True)
    G = Fx.G
    wb = P.sb([128, 8, 192], BF16); bwb = Buf()
    stage = [P.sb([128, 192]) for _ in range(2)]; bstage = [Buf() for _ in range(2)]
    load_cast_weight(P, D_["w_all"], 8, 192, wb, bwb, stage, bstage, col0=0, cw=192)
    bcv = Buf()
    cw = P.sb([128, 3, 3]); cb = P.sb([128, 3]); hbias = P.sb([128, 1]); hs = P.sb([128, 512])
    P.dma("sync", cw, D_["convw"], w=[bcv]); P.dma("sync", cb, D_["convb"], w=[bcv]); P.dma("sync", hbias, D_["hbias"], w=[bcv])
    P.dma("sync", hs, hs_d, w=[bcv])
    u = P.sb([128, 3, S]); bu = [Buf() for _ in range(3)]
    uc = P.sb([128, 3, LC]); buc = [Buf() for _ in range(3)]
    ht = [P.sb([128, 8, 512], BF16) for _ in range(2)]; bht = [Buf() for _ in range(2)]
    pp0 = P.ps([128, 512]); pp = [pp0, pp0]; bpp0 = Buf(); bpp = [bpp0, bpp0]
    kq = 0
    tiles = [(t0, 512, False) for t0 in range(0, S, 512)] + [(0, LC, True)]
    for (t0, T, isc) in tiles:
        for b in range(2):
            src = D_["hTc"][b] if isc else D_["hTl"][b, :, :, t0:t0 + T]
            P.dma("sync" if b else "gpsimd", ht[b][:, :, :T], src, w=[bht[b]])
        for s3 in range(3):
            q = kq % 2
            kq += 1
            for b in range(2):
                for kc in range(8):
                    P.op("tensor", lambda e, q=q, b=b, kc=kc, s3=s3, T=T: e.matmul(pp[q][b * 64:(b + 1) * 64, :T], lhsT=wb[:, kc, s3 * 64:(s3 + 1) * 64],
                                                                            rhs=ht[b][:, kc, :T], start=(kc == 0), stop=(kc == 7)),
                         r=[bwb, bht[b]], w=[bpp[q]])
            dst = uc[:, s3, :] if isc else u[:, s3, t0:t0 + T]
            bd = buc[s3] if isc else bu[s3]
            if s3 % 2:
                P.op("vector", lambda e, q=q, dst=dst, T=T: e.tensor_copy(out=dst, in_=pp[q][:, :T]), r=[bpp[q]], w=[bd])
            else:
                P.op("scalar", lambda e, q=q, dst=dst, T=T: e.copy(out=dst, in_=pp[q][:, :T]), r=[bpp[q]], w=[bd])
    tmp = P.sb([128, 2048]); btmp = Buf()
    carry = P.sb([128, 2]); bcar = Buf()
    for s3 in range(3):
        short_conv_inplace(P, u[:, s3, :], bu[s3], S, cw[:, s3, :], cb[:, s3:s3 + 1], bcv, tmp, btmp, carry, bcar)
        short_conv_inplace(P, uc[:, s3, :], buc[s3], LC, cw[:, s3, :], cb[:, s3:s3 + 1], bcv, tmp, btmp, carry, bcar)
    for c4 in range(4):
        sl = slice(c4 * 2048, (c4 + 1) * 2048)
        P.op("vector" if c4 % 2 else "gpsimd", lambda e, sl=sl: e.tensor_tensor(out=u[:, 2, sl], in0=u[:, 2, sl], in1=u[:, 1, sl], op=ALU.mult),
             r=[bu[1], bu[2]], w=[bu[2]])
    P.op("vector", lambda e: e.tensor_tensor(out=uc[:, 2, :], in0=uc[:, 2, :], in1=uc[:, 1, :], op=ALU.mult), r=[buc[1], buc[2]], w=[buc[2]])
    if dbg is not None:
        P.dma("sync", dbg["x0c"], u[:, 0, :], r=[bu[0]])
        P.dma("sync", dbg["z"], u[:, 2, :], r=[bu[2]])
    yc = [P.sb([128, LC]) for _ in range(2)]; byc = [Buf() for _ in range(2)]
    P.op("vector", lambda e: e.memset(yc[0], 0.0), w=[byc[0]])
    P.op("gpsimd", lambda e: e.memset(yc[1], 0.0), w=[byc[1]])
    zc = uc[:, 2, :]
    ii = 0
    for dlag in range(-(LC - 1), LC):
        col = dlag if dlag >= 0 else LC - dlag
        lo = max(0, dlag)
        hi = min(LC, LC + dlag)
        a = ii % 2
        ii += 1
        eng = "vector"
        P.op(eng, lambda e, a=a, lo=lo, hi=hi, dlag=dlag, col=col: e.scalar_tensor_tensor(out=yc[a][:, lo:hi], in0=zc[:, lo - dlag:hi - dlag], scalar=hs[:, col:col + 1],
                                                                                 in1=yc[a][:, lo:hi], op0=ALU.mult, op1=ALU.add),
             r=[buc[2], bcv, byc[a]], w=[byc[a]])
    P.op("vector", lambda e: e.tensor_tensor(out=yc[0], in0=yc[0], in1=yc[1], op=ALU.add), r=[byc[0], byc[1]], w=[byc[0]])
    oc = P.sb([128, LC], BF16); boc = Buf()
    P.op("vector", lambda e: e.scalar_tensor_tensor(out=yc[0], in0=zc, scalar=hbias[:, 0:1], in1=yc[0], op0=ALU.mult, op1=ALU.add),
         r=[buc[2], bcv, byc[0]], w=[byc[0]])
    P.op("vector", lambda e: e.tensor_tensor(out=oc, in0=yc[0], in1=uc[:, 0, :], op=ALU.mult), r=[byc[0], buc[0]], w=[boc])
    toks = [P.dma("sync", mix_hyc, oc, r=[boc])]
    bzs = Buf()
    P.dma("sync", zs_d, u[:, 2, :], r=[bu[2]], w=[bzs])
    zst = [P.sb([128, G, 128]) for _ in range(2)]; bzst = [Buf() for _ in range(2)]
    zg = [P.sb([128, G, 128], BF16) for _ in range(2)]; bzg = [Buf() for _ in range(2)]
    hfg = [P.sb([128, G, 2, 128]) for _ in range(2)]; bhfg = [Buf() for _ in range(2)]
    yst = [P.sb([128, G, 128]) for _ in range(2)]; byst = [Buf() for _ in range(2)]
    bys = Buf()
    d = Fx.dft
    twr = Fx.tw[:, 0:1, :].to_broadcast([128, G, 128])
    twi = Fx.tw[:, 1:2, :].to_broadcast([128, G, 128])
    for sg in range(64 // G):
        s = sg % 2
        c0 = sg * G
        for b in range(2):
            P.dma("sync" if b else "gpsimd", zst[s][b * 64:(b + 1) * 64, :, :],
                  zs_d[b * 64 + c0:b * 64 + c0 + G, :].rearrange("c (n1 n2) -> n1 c n2", n2=128), r=[bzs], w=[bzst[s]])
        P.dma("sync", hfg[s], hf_d[:, c0:c0 + G, :, :], w=[bhfg[s]])
        P.op("scalar", lambda e, s=s: e.copy(out=zg[s], in_=zst[s]), r=[bzst[s]], w=[bzg[s]])
        fft_fwd(P, Fx, zg[s], bzg[s], 0, d[:, 0:2, :].rearrange("p a k -> p (a k)"))
        cmul(P, Fx, Fx.pxr, Fx.pxi, Fx.bpx, hfg[s][:, :, 0, :], hfg[s][:, :, 1, :], bhfg[s], Fx.yr, Fx.yi, Fx.by)
        for g in range(G):
            P.op("tensor", lambda e, g=g: e.matmul(Fx.pb[:, g, :], lhsT=Fx.yr[:, g, :], rhs=d[:, 7:9, :].rearrange("p a k -> p (a k)"), start=True, stop=False),
                 r=[Fx.by, Fx.bc], w=[Fx.bpb])
            P.op("tensor", lambda e, g=g: e.matmul(Fx.pb[:, g, :], lhsT=Fx.yi[:, g, :], rhs=d[:, 9:11, :].rearrange("p a k -> p (a k)"), start=False, stop=True),
                 r=[Fx.by, Fx.bc], w=[Fx.bpb])
        cmul(P, Fx, Fx.pb[:, :, 0:128], Fx.pb[:, :, 128:256], Fx.bpb, twr, twi, Fx.bc, Fx.br, Fx.bi, Fx.bb, conj=True)
        py = Fx.py.rearrange("p g k -> p (g k)")
        P.op("tensor", lambda e: e.matmul(py, lhsT=d[:, 11, :], rhs=Fx.br.rearrange("p g k -> p (g k)"), start=True, stop=False), r=[Fx.bb, Fx.bc], w=[Fx.bpy])
        P.op("tensor", lambda e: e.matmul(py, lhsT=d[:, 12, :], rhs=Fx.bi.rearrange("p g k -> p (g k)"), start=False, stop=True), r=[Fx.bb, Fx.bc], w=[Fx.bpy])
        P.op("scalar", lambda e, s=s: e.activation(out=yst[s], in_=Fx.py, func=AF.Copy, scale=float(1.0 / NF)), r=[Fx.bpy], w=[byst[s]])
        for b in range(2):
            P.dma("gpsimd" if b else "sync", ys_d[b * 64 + c0:b * 64 + c0 + G, :].rearrange("c (n1 n2) -> n1 c n2", n2=128),
                  yst[s][b * 64:(b + 1) * 64, :, :], r=[byst[s]], w=[bys])
    P.dma("sync", u[:, 1, :], ys_d, r=[bys, bu[1]], w=[bu[1]])
    if dbg is not None:
        P.dma("sync", dbg["y"], u[:, 1, :], r=[bu[1]])
    ob = [P.sb([128, 2048], BF16) for _ in range(2)]; bob = [Buf() for _ in range(2)]
    for c4 in range(4):
        sl = slice(c4 * 2048, (c4 + 1) * 2048)
        o = c4 % 2
        P.op("vector", lambda e, sl=sl: e.scalar_tensor_tensor(out=u[:, 1, sl], in0=u[:, 2, sl], scalar=hbias[:, 0:1], in1=u[:, 1, sl], op0=ALU.mult, op1=ALU.add),
             r=[bu[2], bcv, bu[1]], w=[bu[1]])
        P.op("gpsimd", lambda e, sl=sl, o=o: e.tensor_tensor(out=ob[o], in0=u[:, 1, sl], in1=u[:, 0, sl], op=ALU.mult), r=[bu[1], bu[0]], w=[bob[o]])
        toks.append(P.dma("sync", mix_hy[:, sl], ob[o], r=[bob[o]]))
    return toks


def l2_attn_pass(P, nc, D_, mix_at):
    C = Consts(P)
    NTK = S + LC
    wb = P.sb([128, 8, 320], BF16); bwb = Buf()
    stage = [P.sb([128, 320]) for _ in range(2)]; bstage = [Buf() for _ in range(2)]
    load_cast_weight(P, D_["w_all"], 8, 320, wb, bwb, stage, bstage, col0=192, cw=320)
    bcv = Buf()
    mask = P.sb([128, 384]); sink = P.sb([128, 1])
    P.dma("sync", mask, D_["maskband"], w=[bcv]); P.dma("sync", sink, D_["sink"], w=[bcv])
    qT = P.sb([64, 2, NTK], BF16); bq = [Buf() for _ in range(2)]
    kT = P.sb([64, 2, NTK], BF16); bk = [Buf() for _ in range(2)]
    vS = P.sb([128, 2, NTK // 128, 64], BF16); bv = [Buf() for _ in range(2)]
    oT = P.sb([64, 2, NTK], BF16); bo = [Buf() for _ in range(2)]
    ht = [P.sb([128, 8, 512], BF16) for _ in range(2)]; bht = [Buf() for _ in range(2)]
    rp = P.sb([64, 2, 512]); brp = Buf()
    pq = [P.ps([64, 512]) for _ in range(2)]; bpq = [Buf() for _ in range(2)]
    pv = P.ps([128, 4, 64]); bpv = Buf()
    t1 = P.sb([64, 512]); bt1 = Buf()
    t2 = P.sb([64, 512]); bt2 = Buf()
    tiles = [(t0, 512, False) for t0 in range(0, S, 512)] + [(0, LC, True)]
    for (t0, T, isc) in tiles:
        o0 = S if isc else t0
        if not isc:
            P.dma("sync", rp, D_["rope"][:, :, t0:t0 + T], w=[brp])
        for b in range(2):
            src = D_["hTc"][b] if isc else D_["hTl"][b, :, :, t0:t0 + T]
            P.dma("sync" if b else "gpsimd", ht[b][:, :, :T], src, w=[bht[b]])
        for b in range(2):
            for (dst, bd, c0) in ((qT, bq[b], 0), (kT, bk[b], 128)):
                for half in range(1 if isc else 2):
                    for kc in range(8):
                        P.op("tensor", lambda e, half=half, kc=kc, c0=c0, b=b, T=T: e.matmul(pq[half][:, :T], lhsT=wb[:, kc, c0 + half * 64:c0 + (half + 1) * 64],
                                                                                    rhs=ht[b][:, kc, :T], start=(kc == 0), stop=(kc == 7)),
                             r=[bwb, bht[b]], w=[bpq[half]])
                if isc:
                    P.op("scalar", lambda e, dst=dst, b=b, o0=o0, T=T: e.copy(out=dst[:, b, o0:o0 + T], in_=pq[0][:, :T]), r=[bpq[0]], w=[bd])
                else:
                    P.op("vector", lambda e, T=T: e.tensor_tensor(out=t1[:, :T], in0=pq[0][:, :T], in1=rp[:, 0, :T], op=ALU.mult), r=[bpq[0], brp], w=[bt1])
                    P.op("vector", lambda e, T=T: e.tensor_tensor(out=t2[:, :T], in0=pq[1][:, :T], in1=rp[:, 1, :T], op=ALU.mult), r=[bpq[1], brp], w=[bt2])
                    P.op("gpsimd", lambda e, dst=dst, b=b, o0=o0, T=T: e.tensor_tensor(out=dst[:, b, o0:o0 + T], in0=t1[:, :T], in1=t2[:, :T], op=ALU.add),
                         r=[bt1, bt2], w=[bd])
            nb = T // 128
            for blk in range(nb):
                for kc in range(8):
                    P.op("tensor", lambda e, blk=blk, kc=kc, b=b: e.matmul(pv[:, blk, :], lhsT=ht[b][:, kc, blk * 128:(blk + 1) * 128], rhs=wb[:, kc, 256:320],
                                                                        start=(kc == 0), stop=(kc == 7)),
                         r=[bwb, bht[b]], w=[bpv])
            P.op("scalar", lambda e, b=b, o0=o0, nb=nb: e.copy(out=vS[:, b, o0 // 128:o0 // 128 + nb, :], in_=pv[:, :nb, :]), r=[bpv], w=[bv[b]])
    ps_ = [P.ps([128, 1024]) for _ in range(1)]; bps = [Buf() for _ in range(1)]
    ptr = P.ps([128, 8, 128]); bptr = Buf()
    po = P.ps([64, 128]); bpo = Buf()
    sc = [P.sb([128, 640]) for _ in range(2)]; bsc = [Buf() for _ in range(2)]
    pn = [P.sb([128, 640]) for _ in range(2)]; bpn = [Buf() for _ in range(2)]
    pT = [P.sb([128, 5, 128], BF16) for _ in range(2)]; bpT = [Buf() for _ in range(2)]
    st = [P.sb([128, 8]) for _ in range(2)]; bst = [Buf() for _ in range(2)]
    SCALE = 0.125
    ib = 0
    for b in range(2):
        blocks = [(n, False) for n in range(S // 128)] + [(n, True) for n in range(LC // 128)]
        for (n, isc) in blocks:
            a = ib % 2
            ib += 1
            q0 = (S if isc else 0) + n * 128
            if isc:
                kb0, kb1 = 0, 0
            else:
                kb0, kb1 = max(0, n - 1), min(S // 128, n + 2)
            nk = (kb1 - kb0) * 128
            mo = (kb0 - (n - 1)) * 128 if not isc else 0
            tot = nk + LC
            psl = ps_[0]
            if nk:
                P.op("tensor", lambda e, b=b, q0=q0, kb0=kb0, nk=nk: e.matmul(psl[:, 0:nk], lhsT=qT[:, b, q0:q0 + 128], rhs=kT[:, b, kb0 * 128:kb0 * 128 + nk], start=True, stop=True),
                     r=[bq[b], bk[b]], w=[bps[0]])
            P.op("tensor", lambda e, b=b, q0=q0: e.matmul(psl[:, 512:512 + LC], lhsT=qT[:, b, q0:q0 + 128], rhs=kT[:, b, S:S + LC], start=True, stop=True),
                 r=[bq[b], bk[b]], w=[bps[0]])
            if nk:
                P.op("vector", lambda e, a=a, nk=nk, mo=mo: e.scalar_tensor_tensor(out=sc[a][:, 0:nk], in0=psl[:, 0:nk], scalar=SCALE, in1=mask[:, mo:mo + nk],
                                                                                op0=ALU.mult, op1=ALU.add), r=[bps[0], bcv], w=[bsc[a]])
            P.op("scalar", lambda e, a=a, nk=nk: e.activation(out=sc[a][:, nk:nk + LC], in_=psl[:, 512:512 + LC], func=AF.Copy, scale=SCALE), r=[bps[0]], w=[bsc[a]])
            P.op("vector", lambda e, a=a, tot=tot: e.reduce_max(out=st[a][:, 0:1], in_=sc[a][:, :tot], axis=AX.X), r=[bsc[a]], w=[bst[a]])
            P.op("vector", lambda e, a=a: e.tensor_scalar(out=st[a][:, 1:2], in0=st[a][:, 0:1], scalar1=-1.0, scalar2=None, op0=ALU.mult), r=[bst[a]], w=[bst[a]])
            P.op("scalar", lambda e, a=a, tot=tot: e.activation(out=pn[a][:, :tot], in_=sc[a][:, :tot], func=AF.Exp, bias=st[a][:, 1:2], scale=1.0, accum_out=st[a][:, 2:3]),
                 r=[bsc[a], bst[a]], w=[bpn[a], bst[a]])
            P.op("scalar", lambda e, a=a: e.activation(out=st[a][:, 3:4], in_=st[a][:, 1:2], func=AF.Exp, bias=sink[:, 0:1], scale=1.0), r=[bst[a], bcv], w=[bst[a]])
            P.op("vector", lambda e, a=a: e.tensor_tensor(out=st[a][:, 4:5], in0=st[a][:, 2:3], in1=st[a][:, 3:4], op=ALU.add), r=[bst[a]], w=[bst[a]])
            P.op("vector", lambda e, a=a: e.reciprocal(out=st[a][:, 5:6], in_=st[a][:, 4:5]), r=[bst[a]], w=[bst[a]])
            P.op("vector", lambda e, a=a, tot=tot: e.tensor_scalar(out=pn[a][:, :tot], in0=pn[a][:, :tot], scalar1=st[a][:, 5:6], scalar2=None, op0=ALU.mult),
                 r=[bpn[a], bst[a]], w=[bpn[a]])
            nkb = tot // 128
            for kb in range(nkb):
                P.op("tensor", lambda e, a=a, kb=kb: e.transpose(out=ptr[:, kb, :], in_=pn[a][:, kb * 128:(kb + 1) * 128], identity=C.identf), r=[bpn[a], C.b], w=[bptr])
            P.op("scalar", lambda e, a=a, nkb=nkb: e.copy(out=pT[a][:, :nkb, :], in_=ptr[:, :nkb, :]), r=[bptr], w=[bpT[a]])
            for kb in range(nkb):
                vblk = (kb0 + kb) if kb < nk // 128 else (S // 128 + kb - nk // 128)
                P.op("tensor", lambda e, a=a, kb=kb, vblk=vblk, b=b, nkb=nkb: e.matmul(po, lhsT=vS[:, b, vblk, :], rhs=pT[a][:, kb, :], start=(kb == 0), stop=(kb == nkb - 1)),
                     r=[bv[b], bpT[a]], w=[bpo])
            P.op("vector", lambda e, b=b, q0=q0: e.tensor_copy(out=oT[:, b, q0:q0 + 128], in_=po), r=[bpo], w=[bo[b]])
    return [P.dma("sync", mix_at, oT, r=[bo[0], bo[1]])]


def build_L2(debug=False, parts=("filter", "hyena", "attn")):
    nc = bass.Bass("TRN2", target_bir_lowering=False)
    D_ = {}

    def din(name, shape, dt=F32):
        D_[name] = nc.dram_tensor(name, list(shape), dt, kind="ExternalInput").ap()
    din("zembT_big", [33, NF]); din("t_big", [64, NF]); din("zembT_small", [33, 512]); din("t_small", [128, 512])
    din("dftm", [128, 13, 128]); din("twid", [128, 2, 128]); din("rope", [64, 2, S]); din("maskband", [128, 384])
    din("w_all", [D, 512]); din("convw", [128, 3, 3]); din("convb", [128, 3]); din("hbias", [128, 1]); din("decay", [128, 2])
    din("hw1", [33, 64]); din("hw2", [64, 64]); din("hw3", [64, 2, 128]); din("hvec", [64, 3]); din("sink", [128, 1])
    din("hTl", [2, 128, 8, S], BF16); din("hTc", [2, 128, 8, LC], BF16)
    hf_d = nc.dram_tensor("hf_d", [128, 64, 2, 128], F32).ap()
    hs_d = nc.dram_tensor("hs_d", [128, 512], F32).ap()
    hb_d = nc.dram_tensor("hb_d", [64, NF], F32).ap()
    zs_d = nc.dram_tensor("zs_d", [128, S], F32).ap()
    ys_d = nc.dram_tensor("ys_d", [128, S], F32).ap()
    mix_hy = nc.dram_tensor("mix_hy", [128, S], BF16, kind="ExternalOutput").ap()
    mix_hyc = nc.dram_tensor("mix_hyc", [128, LC], BF16, kind="ExternalOutput").ap()
    mix_at = nc.dram_tensor("mix_at", [64, 2, S + LC], BF16, kind="ExternalOutput").ap()
    dbg = None
    if debug:
        dbg = {}
        for nm in ("x0c", "z", "y"):
            dbg[nm] = nc.dram_tensor("dbg_" + nm, [128, S], F32, kind="ExternalOutput").ap()
    P = Prog(nc)
    toks = []
    if "filter" in parts:
        toks += l2_filter_pass(P, nc, D_, hf_d, hs_d, hb_d, None)
        P.barrier()
    if "hyena" in parts:
        toks += l2_hyena_pass(P, nc, D_, hf_d, hs_d, zs_d, ys_d, mix_hy, mix_hyc, dbg)
        P.barrier()
    if "attn" in parts:
        toks += l2_attn_pass(P, nc, D_, mix_at)
    P.finish(toks)
    return nc


NTR = LC + S
NCH = NTR // 128


def l4_consts():
    inv = (10000.0 ** (-np.linspace(0.0, 1.0, 128, dtype=np.float32))).astype(np.float32)
    posF = np.arange(NTR, dtype=np.float32)
    posB = np.concatenate([LC - 1 - np.arange(LC), LC + (S - 1 - np.arange(S))]).astype(np.float32)
    out = {}
    for nm, pos in (("ropeF", posF), ("ropeB", posB)):
        ang = (pos[None, :] * inv[:, None]).astype(np.float32)
        out[nm] = np.ascontiguousarray(np.stack([np.cos(ang), np.sin(ang)], 1)).astype(np.float32)
    return out


def prep_L4(inp, hT1_b):
    c = l4_consts()
    w = inp["od_w_in"][0]
    maps = []
    for i in range(NCORES):
        b, hd = i // 4, i % 4
        cols = np.concatenate([hd * 256 + np.arange(256), 1024 + hd * 256 + np.arange(256), 2048 + hd * 512 + np.arange(512),
                               4096 + hd * 512 + np.arange(512), 6144 + hd * 512 + np.arange(512)])
        m = dict(c)
        m["w_ret"] = np.ascontiguousarray(w[:, cols])
        m["lrate"] = np.ascontiguousarray(np.broadcast_to(inp["ret_log_rate"][0][:, hd][None, :], (128, 2))).astype(np.float32)
        m["hT"] = hT1_b[b]
        maps.append(m)
    return maps


def build_L4(debug=False):
    nc = bass.Bass("TRN2", target_bir_lowering=False)
    hT = nc.dram_tensor("hT", [128, 8, NTR], BF16, kind="ExternalInput").ap()
    w_ret = nc.dram_tensor("w_ret", [D, 2048], F32, kind="ExternalInput").ap()
    lrate = nc.dram_tensor("lrate", [128, 2], F32, kind="ExternalInput").ap()
    ropes = [nc.dram_tensor("ropeF", [128, 2, NTR], F32, kind="ExternalInput").ap(),
             nc.dram_tensor("ropeB", [128, 2, NTR], F32, kind="ExternalInput").ap()]
    mixT = nc.dram_tensor("mixT", [128, 4, S], BF16, kind="ExternalOutput").ap()
    P = Prog(nc)
    dbg = None
    if debug:
        dbg = {"consts": nc.dram_tensor("dbg_consts", [128, 2, 1024], F32, kind="ExternalOutput").ap(),
               "yf": nc.dram_tensor("dbg_yf", [128, S // 128, 512], BF16, kind="ExternalOutput").ap()}
    toks = l4_pass(P, nc, hT, w_ret, lrate, ropes, mixT, dbg)
    P.finish(toks)
    return nc


def l4_pass(P, nc, hT, w_ret, lrate, ropes, mixT, dbg=None):
    C = Consts(P)
    I32 = mybir.dt.int32
    wb = P.sb([128, 8, 2048], BF16); bwb = Buf()
    stage = [P.sb([128, 1024]) for _ in range(2)]; bstage = [Buf() for _ in range(2)]
    load_cast_weight(P, w_ret, 8, 2048, wb, bwb, stage, bstage)
    bcv = Buf()
    lr = P.sb([128, 2]); P.dma("sync", lr, lrate, w=[bcv])
    lg = P.sb([128, 2])
    P.op("scalar", lambda e: e.activation(out=lg, in_=lr, func=AF.Exp), r=[bcv], w=[bcv])
    P.op("vector", lambda e: e.tensor_scalar(out=lg, in0=lg, scalar1=-1.0, scalar2=None, op0=ALU.mult), r=[bcv], w=[bcv])
    epsb = P.sb([128, 1]); P.op("gpsimd", lambda e: e.memset(epsb, EPS), w=[bcv])
    diffi = P.sb([128, 128]).bitcast(I32)
    diff = P.sb([128, 128])
    P.op("gpsimd", lambda e: e.iota(diffi, pattern=[[1, 128]], base=0, channel_multiplier=-1), w=[bcv])
    P.op("vector", lambda e: e.tensor_copy(out=diff, in_=diffi), r=[bcv], w=[bcv])
    coli = P.sb([128, 128]).bitcast(I32); colf = P.sb([128, 128])
    P.op("gpsimd", lambda e: e.iota(coli, pattern=[[1, 128]], base=0, channel_multiplier=0), w=[bcv])
    P.op("vector", lambda e: e.tensor_copy(out=colf, in_=coli), r=[bcv], w=[bcv])
    rowi = P.sb([128, 1]).bitcast(I32); rowf = P.sb([128, 1])
    P.op("gpsimd", lambda e: e.iota(rowi, pattern=[[0, 1]], base=0, channel_multiplier=1), w=[bcv])
    P.op("vector", lambda e: e.tensor_copy(out=rowf, in_=rowi), r=[bcv], w=[bcv])
    dmask = [P.sb([128, 128]) for _ in range(2)]
    xi = [P.sb([128, 4, 128]) for _ in range(2)]
    zeta = [P.sb([128, 1]) for _ in range(2)]
    gch = [P.sb([128, 1]) for _ in range(2)]
    tA = P.sb([128, 128]); tB = P.sb([128, 128]); tc1 = P.sb([128, 1])
    for dr in range(2):
        sgn = 1.0 if dr == 0 else -1.0
        lgd = lg[:, dr:dr + 1]
        P.op("vector", lambda e, sgn=sgn: e.tensor_scalar(out=tA, in0=diff, scalar1=sgn, scalar2=0.0, op0=ALU.mult, op1=ALU.max), r=[bcv], w=[bcv])
        P.op("scalar", lambda e, dr=dr, lgd=lgd: e.activation(out=dmask[dr], in_=tA, func=AF.Exp, scale=lgd), r=[bcv], w=[bcv])
        P.op("vector", lambda e, sgn=sgn: e.tensor_scalar(out=tB, in0=diff, scalar1=sgn, scalar2=0.0, op0=ALU.mult, op1=ALU.is_ge), r=[bcv], w=[bcv])
        P.op("vector", lambda e, dr=dr: e.tensor_tensor(out=dmask[dr], in0=dmask[dr], in1=tB, op=ALU.mult), r=[bcv], w=[bcv])
        if dr == 0:
            P.op("vector", lambda e: e.tensor_scalar(out=tA, in0=colf, scalar1=1.0, scalar2=None, op0=ALU.add), r=[bcv], w=[bcv])
        else:
            P.op("vector", lambda e: e.tensor_scalar(out=tA, in0=colf, scalar1=-1.0, scalar2=128.0, op0=ALU.mult, op1=ALU.add), r=[bcv], w=[bcv])
        for rr in range(4):
            P.op("scalar", lambda e, dr=dr, rr=rr, lgd=lgd: e.activation(out=xi[dr][:, rr, :], in_=tA, func=AF.Exp, scale=lgd), r=[bcv], w=[bcv])
        if dr == 0:
            P.op("vector", lambda e: e.tensor_scalar(out=tc1, in0=rowf, scalar1=-1.0, scalar2=127.0, op0=ALU.mult, op1=ALU.add), r=[bcv], w=[bcv])
        else:
            P.op("vector", lambda e: e.tensor_copy(out=tc1, in_=rowf), r=[bcv], w=[bcv])
        P.op("scalar", lambda e, dr=dr, lgd=lgd: e.activation(out=zeta[dr], in_=tc1, func=AF.Exp, scale=lgd), r=[bcv], w=[bcv])
        P.op("gpsimd", lambda e: e.memset(tc1, 128.0), r=[bcv], w=[bcv])
        P.op("scalar", lambda e, dr=dr, lgd=lgd: e.activation(out=gch[dr], in_=tc1, func=AF.Exp, scale=lgd), r=[bcv], w=[bcv])
    if dbg is not None:
        for dr in range(2):
            P.dma("sync", dbg["consts"][:, dr, 0:128], dmask[dr], r=[bcv])
            P.dma("sync", dbg["consts"][:, dr, 128:640], xi[dr].rearrange("p a b -> p (a b)"), r=[bcv])
            P.dma("sync", dbg["consts"][:, dr, 640:641], zeta[dr], r=[bcv], allow_slow_non_contiguous=True)
            P.dma("sync", dbg["consts"][:, dr, 641:642], gch[dr], r=[bcv], allow_slow_non_contiguous=True)
            P.dma("sync", dbg["consts"][:, dr, 642:644], lg, r=[bcv])
    ht = [P.sb([128, 8, 512], BF16) for _ in range(2)]; bht = [Buf() for _ in range(2)]
    rp = [P.sb([128, 2, 512]) for _ in range(2)]; brp = [Buf() for _ in range(2)]
    qr = P.sb([128, 2, 512], BF16); bqr = Buf()
    qx = P.sb([128, 2, 512], BF16); bqx = Buf()
    kr = P.sb([128, 2, 512], BF16); bkr = Buf()
    k32 = P.sb([128, 2, 512]); bk32 = Buf()
    q32 = P.sb([128, 2, 512]); bq32 = Buf()
    vt = P.sb([128, 4, 512], BF16); bvt = Buf()
    sg = [P.sb([128, 512]) for _ in range(2)]; bsg = [Buf() for _ in range(2)]
    yacc = P.sb([128, S // 128, 512], BF16); byacc = Buf()
    R = P.sb([128, 2, 512]); bR = Buf()
    Rb = P.sb([128, 2, 512], BF16); bRb = Buf()
    im = P.sb([128, 128], BF16); bim = Buf()
    kz = P.sb([128, 256], BF16); bkz = Buf()
    st = [P.sb([128, 4]) for _ in range(2)]; bst = [Buf() for _ in range(2)]
    junk = P.sb([128, 512]); bjunk = Buf()
    yt = P.sb([128, 512]); byt = Buf()
    mo = [P.sb([128, 4, 128], BF16) for _ in range(2)]; bmo = [Buf() for _ in range(2)]
    t1 = P.sb([128, 512]); bt1 = Buf()
    t2 = P.sb([128, 512]); bt2 = Buf()
    pqk = [P.ps([128, 512]) for _ in range(2)]; bpqk = [Buf() for _ in range(2)]
    pvg = [P.ps([128, 512]) for _ in range(2)]; bpvg = [Buf() for _ in range(2)]
    pin = P.ps([128, 4, 128]); bpin = Buf()
    po = P.ps([128, 512]); bpo = Buf()
    pR = P.ps([128, 512]); bpR = Buf()
    ptr = P.ps([128, 4, 128]); bptr = Buf()
    toks = []
    for dr in range(2):
        P.op("gpsimd", lambda e: e.memset(R, 0.0), r=[], w=[bR])
        P.op("gpsimd", lambda e: e.memset(Rb, 0.0), r=[], w=[bRb])
        if dr == 0:
            tiles = [(0, LC)] + [(LC + 512 * t, 512) for t in range(S // 512)]
        else:
            tiles = [(0, LC)] + [(LC + 512 * t, 512) for t in reversed(range(S // 512))]
        gcol = 1024 + 512 * dr
        if dbg is not None and dr == 1:
            P.dma("sync", dbg["yf"], yacc, r=[byacc])
        for ti, (t0, T) in enumerate(tiles):
            s = ti % 2
            isc = (t0 == 0)
            P.dma("sync", ht[s][:, :, :T], hT[:, :, t0:t0 + T], w=[bht[s]])
            P.dma("gpsimd", rp[s][:, :, :T], ropes[dr][:, :, t0:t0 + T], w=[brp[s]])
            for (which, c0) in (("q", 0), ("k", 256)):
                if which == "q" and isc:
                    continue
                for dc in range(2):
                    for kc in range(8):
                        P.op("tensor", lambda e, dc=dc, kc=kc, c0=c0, s=s, T=T: e.matmul(pqk[dc][:, :T], lhsT=wb[:, kc, c0 + dc * 128:c0 + (dc + 1) * 128], rhs=ht[s][:, kc, :T],
                                                                                   start=(kc == 0), stop=(kc == 7)), r=[bwb, bht[s]], w=[bpqk[dc]])
                cosT = rp[s][:, 0, :T]
                sinT = rp[s][:, 1, :T]
                sc_ = 1.0 if which == "q" else 1.0 / 16.0
                o32 = q32 if which == "q" else k32
                bo32 = bq32 if which == "q" else bk32
                P.op("vector", lambda e, T=T, cosT=cosT, sc_=sc_: e.scalar_tensor_tensor(out=t1[:, :T], in0=pqk[0][:, :T], scalar=sc_, in1=cosT, op0=ALU.mult, op1=ALU.mult), r=[bpqk[0], brp[s]], w=[bt1])
                P.op("vector", lambda e, T=T, sinT=sinT, sc_=sc_: e.scalar_tensor_tensor(out=t2[:, :T], in0=pqk[1][:, :T], scalar=sc_, in1=sinT, op0=ALU.mult, op1=ALU.mult), r=[bpqk[1], brp[s]], w=[bt2])
                P.op("gpsimd", lambda e, T=T, o32=o32: e.tensor_tensor(out=o32[:, 0, :T], in0=t1[:, :T], in1=t2[:, :T], op=ALU.subtract), r=[bt1, bt2], w=[bo32])
                P.op("vector", lambda e, T=T, cosT=cosT, sc_=sc_: e.scalar_tensor_tensor(out=t1[:, :T], in0=pqk[1][:, :T], scalar=sc_, in1=cosT, op0=ALU.mult, op1=ALU.mult), r=[bpqk[1], brp[s]], w=[bt1])
                P.op("vector", lambda e, T=T, sinT=sinT, sc_=sc_: e.scalar_tensor_tensor(out=t2[:, :T], in0=pqk[0][:, :T], scalar=sc_, in1=sinT, op0=ALU.mult, op1=ALU.mult), r=[bpqk[0], brp[s]], w=[bt2])
                P.op("gpsimd", lambda e, T=T, o32=o32: e.tensor_tensor(out=o32[:, 1, :T], in0=t1[:, :T], in1=t2[:, :T], op=ALU.add), r=[bt1, bt2], w=[bo32])
                if which == "q":
                    P.op("scalar", lambda e, T=T: e.copy(out=qr[:, :, :T], in_=q32[:, :, :T]), r=[bq32], w=[bqr])
                    for dc in range(2):
                        P.op("gpsimd", lambda e, T=T, dc=dc, dr=dr: e.tensor_tensor(out=qx[:, dc, :T], in0=q32[:, dc, :T], in1=xi[dr].rearrange("p a b -> p (a b)")[:, :T], op=ALU.mult),
                             r=[bq32, bcv], w=[bqx])
                else:
                    P.op("scalar", lambda e, T=T: e.copy(out=kr[:, :, :T], in_=k32[:, :, :T]), r=[bk32], w=[bkr])
            nchk = T // 128
            for c in range(nchk):
                pp_ = c % 2
                for kc in range(8):
                    P.op("tensor", lambda e, c=c, kc=kc, s=s, pp_=pp_: e.matmul(pvg[pp_], lhsT=ht[s][:, kc, c * 128:(c + 1) * 128], rhs=wb[:, kc, 512:1024], start=(kc == 0), stop=(kc == 7)),
                         r=[bwb, bht[s]], w=[bpvg[pp_]])
                P.op("scalar", lambda e, c=c, pp_=pp_: e.copy(out=vt[:, c, :], in_=pvg[pp_]), r=[bpvg[pp_]], w=[bvt])
            order = list(range(nchk)) if dr == 0 else list(reversed(range(nchk)))
            for c in order:
                cs = slice(c * 128, (c + 1) * 128)
                a = c % 2
                gchunk = (t0 + c * 128 - LC) // 128
                if not isc:
                    for kc in range(8):
                        P.op("tensor", lambda e, kc=kc, s=s, cs=cs, a=a, gcol=gcol: e.matmul(pvg[a], lhsT=ht[s][:, kc, cs], rhs=wb[:, kc, gcol:gcol + 512], start=(kc == 0), stop=(kc == 7)),
                             r=[bwb, bht[s]], w=[bpvg[a]])
                    P.op("scalar", lambda e, a=a: e.activation(out=sg[a], in_=pvg[a], func=AF.Silu), r=[bpvg[a]], w=[bsg[a]])
                    for dc in range(2):
                        P.op("tensor", lambda e, dc=dc, cs=cs: e.matmul(pin[:, 0, :], lhsT=kr[:, dc, cs], rhs=qr[:, dc, cs], start=(dc == 0), stop=(dc == 1)), r=[bkr, bqr], w=[bpin])
                    P.op("vector", lambda e, dr=dr: e.tensor_tensor(out=im, in0=pin[:, 0, :], in1=dmask[dr], op=ALU.mult), r=[bpin, bcv], w=[bim])
                    P.op("tensor", lambda e, c=c: e.matmul(po, lhsT=im, rhs=vt[:, c, :], start=True, stop=False), r=[bim, bvt], w=[bpo])
                    for dc in range(2):
                        P.op("tensor", lambda e, dc=dc, cs=cs: e.matmul(po, lhsT=qx[:, dc, cs], rhs=Rb[:, dc, :], start=False, stop=(dc == 1)), r=[bqx, bRb], w=[bpo])
                for dc in range(2):
                    P.op("tensor", lambda e, dc=dc, cs=cs: e.transpose(out=pin[:, 1 + dc, :], in_=k32[:, dc, cs], identity=C.identf), r=[bk32, C.b], w=[bpin])
                P.op("vector", lambda e, dr=dr: e.tensor_scalar(out=kz, in0=pin[:, 1:3, :].rearrange("p a b -> p (a b)"), scalar1=zeta[dr][:, 0:1], scalar2=None, op0=ALU.mult),
                     r=[bpin, bcv], w=[bkz])
                for dc in range(2):
                    P.op("tensor", lambda e, dc=dc, c=c: e.matmul(pR, lhsT=kz[:, dc * 128:(dc + 1) * 128], rhs=vt[:, c, :], start=True, stop=True), r=[bkz, bvt], w=[bpR])
                    P.op("vector", lambda e, dc=dc, dr=dr: e.scalar_tensor_tensor(out=R[:, dc, :], in0=R[:, dc, :], scalar=gch[dr][:, 0:1], in1=pR, op0=ALU.mult, op1=ALU.add),
                         r=[bpR, bcv, bR], w=[bR])
                    P.op("scalar", lambda e, dc=dc: e.copy(out=Rb[:, dc, :], in_=R[:, dc, :]), r=[bR], w=[bRb])
                if not isc:
                    P.op("scalar", lambda e, a=a: e.activation(out=junk, in_=po, func=AF.Square, accum_out=st[a][:, 0:1]), r=[bpo], w=[bjunk, bst[a]])
                    P.op("scalar", lambda e, a=a: e.activation(out=st[a][:, 1:2], in_=st[a][:, 0:1], func=AF.Sqrt, bias=epsb[:, 0:1], scale=1.0 / 512.0), r=[bst[a], bcv], w=[bst[a]])
                    P.op("vector", lambda e, a=a: e.reciprocal(out=st[a][:, 2:3], in_=st[a][:, 1:2]), r=[bst[a]], w=[bst[a]])
                    if dr == 0:
                        P.op("vector", lambda e, a=a, gchunk=gchunk: e.scalar_tensor_tensor(out=yacc[:, gchunk, :], in0=po, scalar=st[a][:, 2:3], in1=sg[a], op0=ALU.mult, op1=ALU.mult),
                             r=[bpo, bst[a], bsg[a]], w=[byacc])
                    else:
                        P.op("vector", lambda e, a=a: e.scalar_tensor_tensor(out=yt, in0=po, scalar=st[a][:, 2:3], in1=sg[a], op0=ALU.mult, op1=ALU.mult),
                             r=[bpo, bst[a], bsg[a]], w=[byt])
                        P.op("gpsimd", lambda e, gchunk=gchunk: e.tensor_tensor(out=yt, in0=yt, in1=yacc[:, gchunk, :], op=ALU.add), r=[byt, byacc], w=[byt])
                        for f4 in range(4):
                            P.op("tensor", lambda e, f4=f4: e.transpose(out=ptr[:, f4, :], in_=yt[:, f4 * 128:(f4 + 1) * 128], identity=C.identf), r=[byt, C.b], w=[bptr])
                        m_ = gchunk % 2
                        P.op("scalar", lambda e, m_=m_: e.copy(out=mo[m_], in_=ptr), r=[bptr], w=[bmo[m_]])
                        toks.append(P.dma("sync", mixT[:, :, gchunk * 128:(gchunk + 1) * 128], mo[m_], r=[bmo[m_]]))
    return toks


def _run(nc, maps):
    res = run_bass_kernel_spmd(nc, maps, core_ids=list(range(NCORES)))
    return res.results


def kernel(**inputs):
    inp = {k: np.ascontiguousarray(np.asarray(v)) for k, v in inputs.items()}
    m1 = prep_L1(inp)
    gT = m1[0]["gT"]
    r1 = _run(build_L1(), m1)
    hTl = np.zeros((2, 128, 8, S), NPBF)
    hTc = np.zeros((2, 128, 8, LC), NPBF)
    for i in range(NCORES):
        b, q = core_bq(i)
        hTl[b][:, :, q * TL:(q + 1) * TL] = r1[i]["hT_o"][:, :, :TL]
        hTc[b][:, :, q * TC:(q + 1) * TC] = r1[i]["hT_o"][:, :, TL:]
    r2 = _run(build_L2(), prep_L2(inp, hTl, hTc))
    w_out0 = inp["ev_w_out"][0]
    perm = np.concatenate([np.concatenate([64 * j + np.arange(64), 512 + 64 * j + np.arange(64)]) for j in range(8)])
    w_out0p = np.ascontiguousarray(w_out0[perm])
    m3 = []
    for i in range(NCORES):
        b, q = core_bq(i)
        mix = np.zeros((128, 8, TT), NPBF)
        for j in range(8):
            mix[0:64, j, :TL] = r2[j]["mix_hy"][b * 64:(b + 1) * 64, q * TL:(q + 1) * TL]
            mix[0:64, j, TL:] = r2[j]["mix_hyc"][b * 64:(b + 1) * 64, q * TC:(q + 1) * TC]
            mix[64:, j, :TL] = r2[j]["mix_at"][:, b, q * TL:(q + 1) * TL]
            mix[64:, j, TL:] = r2[j]["mix_at"][:, b, S + q * TC:S + (q + 1) * TC]
        m3.append({"xT": r1[i]["xT_o"], "mixT": mix, "w_out": w_out0p, "w1": inp["mlp_w1"][0], "w2": inp["mlp_w2"][0],
                   "modT": r1[i]["modT_o"], "gT": gT})
    r3 = _run(build_L35(0, 8, True, False), m3)
    hT1 = []
    for b in range(2):
        h = np.zeros((128, 8, NTR), NPBF)
        for q in range(4):
            i = b * 4 + q
            h[:, :, q * TC:(q + 1) * TC] = r3[i]["hT_o"][:, :, TL:]
            h[:, :, LC + q * TL:LC + (q + 1) * TL] = r3[i]["hT_o"][:, :, :TL]
        hT1.append(h)
    r4 = _run(build_L4(), prep_L4(inp, hT1))
    m5 = []
    for i in range(NCORES):
        b, q = core_bq(i)
        mix = np.zeros((128, 16, TL), NPBF)
        for hd in range(4):
            mix[:, hd * 4:(hd + 1) * 4, :] = r4[b * 4 + hd]["mixT"][:, :, q * TL:(q + 1) * TL]
        m5.append({"xT": np.ascontiguousarray(r3[i]["xT_o"][:, :, :TL]), "mixT": mix, "w_out": inp["od_w_out"][0], "w1": inp["mlp_w1"][1],
                   "w2": inp["mlp_w2"][1], "modT": r1[i]["modT_o"], "gT": gT})
    r5 = _run(build_L35(1, 16, False, True), m5)
    out = np.zeros((2, S, D), np.float32)
    for i in range(NCORES):
        b, q = core_bq(i)
        out[b, q * TL:(q + 1) * TL] = r5[i]["y_out"].transpose(2, 1, 0).reshape(TL, D)
    return out
```
